# Optimizing a Trainium2 kernel written in Bass

```python
import jax, jax.numpy as jnp
from jax import lax
import numpy as np

D_MODEL = 1024
BATCH = 8
SEQ = 4096
DEPTH = 1

ATTN_HEADS = 8
HEAD_DIM = D_MODEL // 16
ATTN_WIDTH = ATTN_HEADS * HEAD_DIM
DILATED_BRANCHES = ((128, 1), (512, 4), (2048, 16))
ATTN_BLOCK = 128
SGU_GROUPS = 8
SGU_GROUP_DIM = D_MODEL // 16
SGU_WIDTH = SGU_GROUPS * SGU_GROUP_DIM
SGU_CHUNK = 128
MIX_WIDTH = ATTN_WIDTH + SGU_WIDTH
IN_WIDTH = 3 * ATTN_WIDTH + 2 * SGU_WIDTH
PEER_HEADS = 8
PEER_KEY_DIM = 256
N_SUB_KEYS = 128
N_EXPERTS = N_SUB_KEYS * N_SUB_KEYS
PEER_TOPK = 16
PEER_TOKEN_BLOCK = 128
EPS = 1e-6
NEG_INF = -1e30

kernel_name = 'hybrid_dilated_attn_sgu_peer_layer'


def rms_norm(x, g):
    xf = x.astype(jnp.float32)
    y = xf * lax.rsqrt(jnp.mean(xf * xf, axis=-1, keepdims=True) + EPS)
    return (y * g.astype(jnp.float32)).astype(x.dtype)


def alibi_slopes(n_heads):
    i = jnp.arange(1, n_heads + 1, dtype=jnp.float32)
    return jnp.exp2(-8.0 * i / n_heads)


def dilated_branch(q, k, v, slopes, window, dilation):
    b, s, h, dh = q.shape
    sub_len = s // dilation
    w_sub = window // dilation
    c = ATTN_BLOCK
    n = -(-sub_len // c)
    padded = n * c

    def to_sub(t):
        t = t.reshape(b, sub_len, dilation, h, dh).transpose(0, 2, 3, 1, 4)
        return jnp.pad(t, ((0, 0), (0, 0), (0, 0), (0, padded - sub_len), (0, 0)))

    def band(t):
        tp = jnp.pad(t, ((0, 0), (0, 0), (0, 0), (c, 0), (0, 0)))
        prev = tp[:, :, :, :padded].reshape(b, dilation, h, n, c, dh)
        cur = t.reshape(b, dilation, h, n, c, dh)
        return jnp.concatenate([prev, cur], axis=4)

    qs, ks, vs = to_sub(q), to_sub(k), to_sub(v)
    qb = qs.reshape(b, dilation, h, n, c, dh)
    kb, vb = band(ks), band(vs)
    scores = jnp.einsum('bdhnqe,bdhnke->bdhnqk', qb, kb).astype(jnp.float32) * (dh ** -0.5)
    qi = jnp.arange(c)[:, None]
    ki = jnp.arange(2 * c)[None, :]
    dist = c + qi - ki
    blk = jnp.arange(n)[:, None, None]
    valid = (dist >= 0) & (dist <= w_sub) & ((blk > 0) | (ki >= c))
    bias = -slopes[:, None, None, None] * (dilation * dist).astype(jnp.float32)
    scores = jnp.where(valid, scores + bias, NEG_INF)
    m = jnp.max(scores, axis=-1)
    p = jnp.exp(scores - m[..., None])
    l = jnp.sum(p, axis=-1)
    o = jnp.einsum('bdhnqk,bdhnke->bdhnqe', p, vb.astype(jnp.float32)) / l[..., None]

    def from_sub(t):
        t = t.reshape((b, dilation, h, padded) + t.shape[5:])[:, :, :, :sub_len]
        perm = (0, 3, 1, 2) + tuple(range(4, t.ndim))
        return t.transpose(perm).reshape((b, s, h) + t.shape[4:])

    return from_sub(o), from_sub(m), from_sub(l)


def dilated_attention(q, k, v, slopes):
    outs = [dilated_branch(q, k, v, slopes, w, d) for (w, d) in DILATED_BRANCHES]
    m_all = jnp.stack([m for (_, m, _) in outs])
    m_max = jnp.max(m_all, axis=0)
    wts = jnp.stack([l * jnp.exp(m - m_max) for (_, m, l) in outs])
    num = sum(wts[i][..., None] * outs[i][0] for i in range(len(outs)))
    return (num / jnp.sum(wts, axis=0)[..., None]).astype(q.dtype)


def spatial_gating(su, sv, sgu_norm_g, sgu_w, sgu_b):
    b, s, _ = su.shape
    nc = s // SGU_CHUNK
    svn = rms_norm(sv.reshape(b, s, SGU_GROUPS, SGU_GROUP_DIM), sgu_norm_g.reshape(SGU_GROUPS, SGU_GROUP_DIM))
    svn = svn.reshape(b, nc, SGU_CHUNK, SGU_GROUPS, SGU_GROUP_DIM)
    causal = jnp.tril(jnp.ones((SGU_CHUNK, SGU_CHUNK), dtype=bool))
    w = jnp.where(causal[None], sgu_w, 0.0).astype(sgu_w.dtype)
    mixed = jnp.einsum('gts,bcsge->bctge', w, svn) + sgu_b.T[:, :, None]
    return su * mixed.reshape(b, s, SGU_WIDTH)


def peer(h, w_query, sub_keys, expert_u, expert_v):
    b, s, d = h.shape
    t = b * s
    hf = h.reshape(t, d)
    q = (hf @ w_query).reshape(t, PEER_HEADS, 2, PEER_KEY_DIM // 2).astype(jnp.float32)
    s1 = jnp.einsum('thk,nk->thn', q[:, :, 0], sub_keys[0].astype(jnp.float32))
    s2 = jnp.einsum('thk,nk->thn', q[:, :, 1], sub_keys[1].astype(jnp.float32))
    v1, i1 = lax.top_k(s1, PEER_TOPK)
    v2, i2 = lax.top_k(s2, PEER_TOPK)
    cand = (v1[..., :, None] + v2[..., None, :]).reshape(t, PEER_HEADS, PEER_TOPK * PEER_TOPK)
    cidx = (i1[..., :, None] * N_SUB_KEYS + i2[..., None, :]).reshape(t, PEER_HEADS, PEER_TOPK * PEER_TOPK)
    best, pos = lax.top_k(cand, PEER_TOPK)
    eidx = jnp.take_along_axis(cidx, pos, axis=-1).reshape(t, PEER_HEADS * PEER_TOPK)
    gates = jax.nn.softmax(best, axis=-1).reshape(t, PEER_HEADS * PEER_TOPK)
    nb = t // PEER_TOKEN_BLOCK

    def block(args):
        hb, eb, gb = args
        u = jnp.take(expert_u, eb, axis=0)
        a = jnp.einsum('cd,ced->ce', hb, u).astype(jnp.float32)
        coef = gb * jax.nn.gelu(a, approximate=False)
        vv = jnp.take(expert_v, eb, axis=0)
        return jnp.einsum('ce,ced->cd', coef.astype(vv.dtype), vv)

    out = lax.map(block, (hf.reshape(nb, PEER_TOKEN_BLOCK, d),
                          eidx.reshape(nb, PEER_TOKEN_BLOCK, PEER_HEADS * PEER_TOPK),
                          gates.reshape(nb, PEER_TOKEN_BLOCK, PEER_HEADS * PEER_TOPK)))
    return out.reshape(b, s, d).astype(h.dtype)


def setup_inputs(seed: int = 0) -> dict:
    key = jax.random.key(seed)
    ks = jax.random.split(key, 16)
    f32 = jnp.float32
    nrm = lambda k, shp: jax.random.normal(k, shp, f32)
    gain = lambda k, shp: 1.0 + 0.01 * nrm(k, shp)
    return {
        'x': nrm(ks[0], (BATCH, SEQ, D_MODEL)),
        'norm1_g': gain(ks[1], (DEPTH, D_MODEL)),
        'w_in': nrm(ks[2], (DEPTH, D_MODEL, IN_WIDTH)) * D_MODEL ** -0.5,
        'q_norm_g': gain(ks[3], (DEPTH, HEAD_DIM)),
        'k_norm_g': gain(ks[4], (DEPTH, HEAD_DIM)),
        'sgu_norm_g': gain(ks[5], (DEPTH, SGU_WIDTH)),
        'sgu_w': nrm(ks[6], (DEPTH, SGU_GROUPS, SGU_CHUNK, SGU_CHUNK)) * SGU_CHUNK ** -0.5,
        'sgu_b': 1.0 + 0.1 * nrm(ks[7], (DEPTH, SGU_GROUPS, SGU_CHUNK)),
        'attn_out_g': gain(ks[8], (DEPTH, ATTN_WIDTH)),
        'sgu_out_g': gain(ks[9], (DEPTH, SGU_WIDTH)),
        'w_out': nrm(ks[10], (DEPTH, MIX_WIDTH, D_MODEL)) * MIX_WIDTH ** -0.5,
        'norm2_g': gain(ks[11], (DEPTH, D_MODEL)),
        'w_query': nrm(ks[12], (DEPTH, D_MODEL, PEER_HEADS * PEER_KEY_DIM)) * D_MODEL ** -0.5,
        'sub_keys': nrm(ks[13], (DEPTH, 2, N_SUB_KEYS, PEER_KEY_DIM // 2)) * (PEER_KEY_DIM // 2) ** -0.5,
        'expert_u': nrm(ks[14], (DEPTH, N_EXPERTS, D_MODEL)) * D_MODEL ** -0.5,
        'expert_v': nrm(ks[15], (DEPTH, N_EXPERTS, D_MODEL)) * 0.1,
    }


def reference(x, norm1_g, w_in, q_norm_g, k_norm_g, sgu_norm_g, sgu_w, sgu_b, attn_out_g, sgu_out_g,
              w_out, norm2_g, w_query, sub_keys, expert_u, expert_v):
    b, s, _ = x.shape
    slopes = alibi_slopes(ATTN_HEADS)
    splits = [ATTN_WIDTH, 2 * ATTN_WIDTH, 3 * ATTN_WIDTH, 3 * ATTN_WIDTH + SGU_WIDTH]
    for layer in range(DEPTH):
        h = rms_norm(x, norm1_g[layer])
        proj = h @ w_in[layer]
        q, k, v, su, sv = jnp.split(proj, splits, axis=-1)
        q = rms_norm(q.reshape(b, s, ATTN_HEADS, HEAD_DIM), q_norm_g[layer])
        k = rms_norm(k.reshape(b, s, ATTN_HEADS, HEAD_DIM), k_norm_g[layer])
        v = v.reshape(b, s, ATTN_HEADS, HEAD_DIM)
        attn = dilated_attention(q, k, v, slopes).reshape(b, s, ATTN_WIDTH)
        gated = spatial_gating(su, sv, sgu_norm_g[layer], sgu_w[layer], sgu_b[layer])
        mixed = jnp.concatenate([rms_norm(attn, attn_out_g[layer]),
                                 rms_norm(gated, sgu_out_g[layer])], axis=-1)
        x = x + mixed @ w_out[layer]
        h2 = rms_norm(x, norm2_g[layer])
        x = x + peer(h2, w_query[layer], sub_keys[layer], expert_u[layer], expert_v[layer])
    return x
```

```python
import contextlib
import numpy as np
import concourse.bass as bass
import concourse.mybir as mybir
from concourse.bass_utils import run_bass_kernel_spmd

F32 = mybir.dt.float32
BF16 = mybir.dt.bfloat16
U32 = mybir.dt.uint32
AF = mybir.ActivationFunctionType
ALU = mybir.AluOpType
AX = mybir.AxisListType

ENGS = ['pe', 'act', 'dve', 'pool', 'sp']
EPOCH = 30000

SEQ = 4096
DM = 1024
NT = SEQ // 128
EPS = 1e-6
TG = 256
NEG = -30000.0


class Sched:
    NS = 16

    def __init__(self, nc):
        self.nc = nc
        self.q = {e: [] for e in ENGS}
        self.cnt = {e: 0 for e in ENGS}
        self.seen = {e: {} for e in ENGS}
        self.lastw = {}
        self.readers = {}
        self.sems = {}
        self.dma_n = {e: 0 for e in ENGS}
        self.dma_slot_val = {}
        self.dma_slot_tok = {}

    def op(self, eng, fn, reads=(), writes=(), dma=False):
        deps = {}

        def add(tok):
            if tok is None:
                return
            k, v = tok
            if deps.get(k, 0) < v:
                deps[k] = v
        for k in reads:
            add(self.lastw.get(k))
        for k in writes:
            add(self.lastw.get(k))
            for t in self.readers.get(k, ()):
                add(t)
        if dma:
            slot = self.dma_n[eng] % self.NS
            self.dma_n[eng] += 1
            skey = "d_%s_%d" % (eng, slot)
            add(self.dma_slot_tok.get(skey))
            val = self.dma_slot_val.get(skey, 0) + 16
            self.dma_slot_val[skey] = val
            tok = (skey, val)
            self.dma_slot_tok[skey] = tok
            inc = 16
        else:
            c = self.cnt[eng]
            self.cnt[eng] += 1
            skey = "c_%s_%d" % (eng, c // EPOCH)
            tok = (skey, c % EPOCH + 1)
            inc = 1
        waits = []
        for k, v in deps.items():
            if eng == 'pe' and k.startswith('c_pe'):
                continue
            if self.seen[eng].get(k, 0) < v:
                self.seen[eng][k] = v
                waits.append((k, v))
        self.q[eng].append((waits, fn, skey, inc))
        for k in reads:
            self.readers.setdefault(k, []).append(tok)
        for k in writes:
            self.lastw[k] = tok
            self.readers[k] = []
        return tok

    def barrier(self):
        toks = list(self.dma_slot_tok.values())
        for e in ENGS:
            c = self.cnt[e]
            if c > 0:
                toks.append(("c_%s_%d" % (e, (c - 1) // EPOCH), (c - 1) % EPOCH + 1))
        for e in ENGS:
            waits = []
            for k, v in toks:
                if self.seen[e].get(k, 0) < v:
                    self.seen[e][k] = v
                    waits.append((k, v))
            if waits:
                self.q[e].append((waits, None, None, 0))

    def final_wait(self, eng):
        waits = list(self.dma_slot_tok.values())
        self.q[eng].append((waits, None, None, 0))

    def emit(self):
        nc = self.nc
        engmap = {'pe': 'tensor', 'act': 'scalar', 'dve': 'vector', 'pool': 'gpsimd', 'sp': 'sync'}
        stack = contextlib.ExitStack()
        keys = []
        for e in ENGS:
            for (waits, fn, skey, inc) in self.q[e]:
                for k, v in waits:
                    if k not in keys:
                        keys.append(k)
                if skey is not None and skey not in keys:
                    keys.append(skey)
        for k in keys:
            self.sems[k] = stack.enter_context(nc.semaphore("s_" + k))
        with stack, nc.Block() as block:
            for e in ENGS:
                items = self.q[e]

                def body(engine, items=items):
                    for (waits, fn, skey, inc) in items:
                        for k, v in waits:
                            engine.wait_ge(self.sems[k], v)
                        if fn is not None:
                            fn().then_inc(self.sems[skey], inc)
                getattr(block, engmap[e])(body)


def build_program(stage=9, dbg=False):
    nc = bass.Bass("TRN2", target_bir_lowering=False)
    S = Sched(nc)
    V, A, T, G, SPQ = nc.vector, nc.scalar, nc.tensor, nc.gpsimd, nc.sync

    def din(name, shape, dt=F32):
        return nc.dram_tensor(name, shape, dt, kind="ExternalInput").ap()

    x_d = din("x", [SEQ, DM])
    w_in_d = din("w_in", [DM, 2560])
    w_out_d = din("w_out", [DM, DM])
    w_q_d = din("w_query", [DM, 2048])
    g1_d = din("g1", [1, DM])
    g2_d = din("g2", [1, DM])
    gqk_d = din("gqk", [1, DM])
    gsv_d = din("gsv", [1, 512])
    gao_d = din("gao", [1, 512])
    gso_d = din("gso", [1, 512])
    wsT_d = din("wsT", [128, 8, 128])
    tril_d = din("trilT", [128, 128])
    bcol_d = din("bcol", [128, 8])
    keysT_d = din("keysT", [128, 2, 128])
    uT_d = din("uT", [128 * 128, DM])
    ev_d = din("ev", [16384, DM])
    bm_d = din("bm", [128, 3 * 2048])
    ident_d = din("ident", [128, 128])
    iota_d = din("iota", [128, 128])
    c16_d = din("c16", [128, 48])
    out_d = nc.dram_tensor("out", [SEQ, DM], F32, kind="ExternalOutput").ap()

    Vs = nc.dram_tensor("Vs", [SEQ, 520], BF16).ap()
    Gs = nc.dram_tensor("Gs", [SEQ, 512], BF16).ap()
    OB = nc.dram_tensor("OB", [3, SEQ, 520], F32).ap()
    UTb = nc.dram_tensor("UTb", [128 * 128, DM], BF16).ap()
    EVb = nc.dram_tensor("EVb", [16384, DM], BF16).ap()
    H2S = nc.dram_tensor("H2S", [NT, 128, DM], BF16).ap()
    QS = nc.dram_tensor("QS", [NT, 128, 2048], BF16).ap()

    BIG = nc.alloc_sbuf_tensor("BIG", [128, 104000], BF16)
    arena = {'segs': None}

    def set_arena(*segs):
        arena['segs'] = [list(sg) for sg in segs]

    def sb(name, shape, dt=F32):
        esz = 2 if dt == BF16 else 4
        n = 1
        for d_ in shape[1:]:
            n *= d_
        nbytes = (n * esz + 31) // 32 * 32
        for sg in arena['segs']:
            if sg[0] + nbytes <= sg[1]:
                off = sg[0]
                sg[0] += nbytes
                break
        else:
            raise RuntimeError("arena full for %s %s" % (name, shape))
        ap = BIG[:, off // 2: off // 2 + n * esz // 2]
        if dt != BF16:
            ap = ap.bitcast(dt)
        if len(shape) == 3:
            ap = ap.rearrange("p (a b) -> p a b", a=shape[1])
        elif len(shape) == 4:
            ap = ap.rearrange("p (a b c) -> p a b c", a=shape[1], b=shape[2])
        return ap

    A_GLOB = (0, 13312)
    A_RA = (13312, 78848)
    A_RC = (78848, 119808)
    A_MIX = (119808, 141312)
    A_WORK = (141312, 208000)
    A_GT = (78848, 144384)
    A_ET = (144384, 168960)
    A_FG = (168960, 208000)
    A_F2 = (119808, 144384)

    PS = nc.alloc_psum_tensor("PS", [128, 4096], F32)

    def bank(b, n=1):
        return PS[:, b * 512:(b + n) * 512]

    def bank_bf(b):
        return bank(b).bitcast(BF16).rearrange("p (a b) -> p a b", a=8)

    def dma(eng, out, in_, reads=(), writes=()):
        q = SPQ if eng == 'sp' else G
        return S.op(eng, lambda: q.dma_start(out=out, in_=in_), reads, writes, dma=True)

    def dve(fn, reads=(), writes=()):
        return S.op('dve', fn, reads, writes)

    def act(fn, reads=(), writes=()):
        return S.op('act', fn, reads, writes)

    def pe(fn, reads=(), writes=()):
        return S.op('pe', fn, reads, writes)

    def pool(fn, reads=(), writes=()):
        return S.op('pool', fn, reads, writes)

    set_arena(A_RA)
    RA = sb("RA", [128, 8, SEQ], BF16)
    qkT = RA
    set_arena(A_RC)
    RC = sb("RC", [128, 20480], BF16)
    W1 = RC.rearrange("p (c n) -> p c n", c=8)
    bm = RC[:, 0:12288].bitcast(F32)
    Wo = RC[:, 0:8192].rearrange("p (c n) -> p c n", c=8)
    Wq = RC[:, 0:16384].rearrange("p (c n) -> p c n", c=8)

    set_arena(A_GLOB)
    ident_f = sb("ident_f", [128, 128])
    ident_b = sb("ident_b", [128, 128], BF16)
    iota_f = sb("iota_f", [128, 128])
    c16 = sb("c16", [128, 48])
    keysT = sb("keysT", [128, 2, 128], BF16)
    eps_t = sb("eps_t", [128, 1])
    ss = sb("ss", [128, 4])
    sd = sb("sd", [128, 4])
    rs = sb("rs", [128, 4])
    ss16 = sb("ss16", [128, 16])
    sd16 = sb("sd16", [128, 16])
    r16 = sb("r16", [128, 16])
    junk = sb("junk", [128, DM], BF16)
    xts = [sb("xt%d" % i, [128, DM]) for i in range(2)]
    set_arena(A_MIX)
    gA = sb("gA", [128, DM])
    gqk = sb("gqk", [128, DM])
    gsv = sb("gsv", [128, 512])
    gao = sb("gao", [128, 512])
    gso = sb("gso", [128, 512])
    wsT_f = sb("wsT_f", [128, 8, 128])
    tril = sb("tril", [128, 128])
    WsT = sb("WsT", [128, 8, 128], BF16)
    bcol = sb("bcol", [128, 8])

    dma('sp', ident_f[:], ident_d, writes=['ident_f'])
    dma('pool', ident_b[:], ident_d, writes=['ident_b'])
    w_in_v = w_in_d.rearrange("(c p) n -> p c n", p=128)
    dma('pool', W1[:, :, 0:1280], w_in_v[:, :, 0:1280], writes=['RC'])
    dma('pool', W1[:, :, 1280:2560], w_in_v[:, :, 1280:2560], reads=['RC'], writes=['RCb'])
    dma('sp', iota_f[:], iota_d, writes=['iota_f'])
    dma('sp', c16[:], c16_d, writes=['c16'])
    dma('sp', gA[:], g1_d.partition_broadcast(128), writes=['gA'])
    dma('sp', gqk[:], gqk_d.partition_broadcast(128), writes=['gqk'])
    dma('sp', gsv[:], gsv_d.partition_broadcast(128), writes=['gsv'])
    dma('sp', gao[:], gao_d.partition_broadcast(128), writes=['gao'])
    dma('sp', gso[:], gso_d.partition_broadcast(128), writes=['gso'])
    dma('sp', wsT_f[:], wsT_d, writes=['wsT_f'])
    dma('sp', tril[:], tril_d, writes=['tril'])
    dma('sp', bcol[:], bcol_d, writes=['bcol'])
    dma('pool', keysT[:], keysT_d, writes=['keysT'])
    dve(lambda: V.memset(eps_t[:], EPS), writes=['eps'])
    dve(lambda: V.tensor_tensor(out=WsT[:], in0=wsT_f[:], in1=tril[:, None, :].to_broadcast([128, 8, 128]),
                                op=ALU.mult), reads=['wsT_f', 'tril'], writes=['WsT'])
    NPIECE = 64
    PROWS = 16384 // NPIECE

    def cast_piece(k):
        if stage < 3:
            return
        rs_ = slice(k * PROWS, (k + 1) * PROWS)
        dma('pool', UTb[rs_, :], uT_d[rs_, :], writes=['UTb%d' % k])
        dma('pool', EVb[rs_, :], ev_d[rs_, :], writes=['EVb%d' % k])

    def rms_rstd(src_ap, src_keys, n, col):
        act(lambda: A.activation(out=junk[:, 0:n], in_=src_ap, func=AF.Square, accum_out=ss[:, col:col + 1]),
            reads=src_keys, writes=['junk', 'ss%d' % col])
        act(lambda: A.activation(out=sd[:, col:col + 1], in_=ss[:, col:col + 1], func=AF.Sqrt,
                                 scale=1.0 / n, bias=eps_t[:, 0:1]),
            reads=['ss%d' % col, 'eps'], writes=['sd%d' % col])
        dve(lambda: V.reciprocal(out=rs[:, col:col + 1], in_=sd[:, col:col + 1]),
            reads=['sd%d' % col], writes=['rs%d' % col])

    set_arena(A_WORK)
    hbs = [sb("hb%d" % i, [128, DM], BF16) for i in range(2)]
    hTts = [sb("hTt%d" % i, [128, 8, 128], BF16) for i in range(2)]
    sq = sb("sq", [128, DM])
    qkn = sb("qkn", [128, DM])
    qkb = sb("qkb", [128, DM], BF16)
    vbs = [sb("vb%d" % i, [128, 8, 65], BF16) for i in range(2)]
    ss8 = sb("ss8", [128, 8])
    sd8 = sb("sd8", [128, 8])
    r8 = sb("r8", [128, 8])
    svn = sb("svn", [128, 8, 64], BF16)
    t1 = sb("t1", [128, 512])
    gated = sb("gated", [128, 512])
    gnb = sb("gnb", [128, 512], BF16)
    for i in range(2):
        dve(lambda i=i: V.memset(vbs[i][:], 1.0), writes=['vb%d' % i])

    def ab_F1(i):
        p = i % 2
        xt, kx, hb, khb, hTt, kh = xts[p], 'xt%d' % p, hbs[p], 'hb%d' % p, hTts[p], 'hTt%d' % p
        rms_rstd(xt[:], [kx], DM, 0)
        dve(lambda: V.scalar_tensor_tensor(out=hb[:], in0=xt[:], scalar=rs[:, 0:1], in1=gA[:],
                                           op0=ALU.mult, op1=ALU.mult),
            reads=[kx, 'rs0', 'gA'], writes=[khb])
        pT = bank_bf(0)
        for c in range(8):
            pe(lambda c=c: T.transpose(out=pT[:, c, :], in_=hb[:, c * 128:(c + 1) * 128], identity=ident_b[:]),
               reads=[khb, 'ident_b'], writes=['b0'])
        act(lambda: A.copy(out=hTt[:], in_=pT), reads=['b0'], writes=[kh])

    def ab_P(i):
        p = i % 2
        hTt, kh = hTts[p], 'hTt%d' % p
        dsts = [1, 2, 3, 4 + 2 * p, 5 + 2 * p]
        for j in range(5):
            for c in range(8):
                pe(lambda j=j, c=c: T.matmul(bank(dsts[j]), lhsT=hTt[:, c, :], rhs=W1[:, c, j * 512:(j + 1) * 512],
                                             start=(c == 0), stop=(c == 7)),
                   reads=[kh, 'RC', 'RCb'], writes=['b%d' % dsts[j]])

    def ab_B1(i):
        p = i % 2
        tsl = slice(i * 128, (i + 1) * 128)
        pqk = bank(1, 2)
        psv = bank(5 + 2 * p)
        ksv = 'b%d' % (5 + 2 * p)
        act(lambda: A.activation(out=sq[:], in_=pqk, func=AF.Square), reads=['b1', 'b2'], writes=['sq'])
        dve(lambda: V.tensor_reduce(out=ss16[:], in_=sq[:].rearrange("p (a b) -> p a b", b=64), axis=AX.X, op=ALU.add),
            reads=['sq'], writes=['ss16'])
        act(lambda: A.activation(out=sd16[:], in_=ss16[:], func=AF.Sqrt, scale=1.0 / 64, bias=eps_t[:, 0:1]),
            reads=['ss16', 'eps'], writes=['sd16'])
        dve(lambda: V.reciprocal(out=r16[:], in_=sd16[:]), reads=['sd16'], writes=['r16'])
        dve(lambda: V.tensor_tensor(out=qkn[:].rearrange("p (a b) -> p a b", b=64),
                                    in0=pqk.rearrange("p (a b) -> p a b", b=64),
                                    in1=r16[:].unsqueeze(2).to_broadcast([128, 16, 64]), op=ALU.mult),
            reads=['b1', 'b2', 'r16'], writes=['qkn'])
        vb = vbs[p]
        kv = 'vb%d' % p
        act(lambda: A.copy(out=vb[:, :, 0:64], in_=bank(3).rearrange("p (h e) -> p h e", e=64)),
            reads=['b3'], writes=[kv])
        dma('pool', Vs[tsl, :], vb[:].rearrange("p h e -> p (h e)"), reads=[kv], writes=['Vs'])
        act(lambda: A.activation(out=sq[:, 0:512], in_=psv, func=AF.Square), reads=[ksv], writes=['sq'])
        dve(lambda: V.tensor_reduce(out=ss8[:], in_=sq[:, 0:512].rearrange("p (a b) -> p a b", b=64), axis=AX.X,
                                    op=ALU.add), reads=['sq'], writes=['ss8'])
        act(lambda: A.activation(out=sd8[:], in_=ss8[:], func=AF.Sqrt, scale=1.0 / 64, bias=eps_t[:, 0:1]),
            reads=['ss8', 'eps'], writes=['sd8'])
        dve(lambda: V.reciprocal(out=r8[:], in_=sd8[:]), reads=['sd8'], writes=['r8'])
        dve(lambda: V.tensor_tensor(out=svn[:], in0=psv.rearrange("p (a b) -> p a b", b=64),
                                    in1=r8[:].unsqueeze(2).to_broadcast([128, 8, 64]), op=ALU.mult),
            reads=[ksv, 'r8'], writes=['svn'])

    def ab_B2(i):
        p = i % 2
        tsl = slice(i * 128, (i + 1) * 128)
        psu, pmx = bank(4 + 2 * p), bank(5 + 2 * p)
        ksu, kmx = 'b%d' % (4 + 2 * p), 'b%d' % (5 + 2 * p)
        dve(lambda: V.tensor_tensor(out=qkb[:], in0=qkn[:], in1=gqk[:], op=ALU.mult),
            reads=['qkn', 'gqk'], writes=['qkb'])
        pT2 = bank_bf(0)
        for c in range(8):
            pe(lambda c=c: T.transpose(out=pT2[:, c, :], in_=qkb[:, c * 128:(c + 1) * 128], identity=ident_b[:]),
               reads=['qkb', 'ident_b'], writes=['b0'])
        act(lambda: A.copy(out=qkT[:, :, tsl], in_=pT2), reads=['b0'], writes=['qkT'])
        for g in range(8):
            pe(lambda g=g: T.matmul(pmx[:, g * 64:(g + 1) * 64], lhsT=WsT[:, g, :], rhs=svn[:, g, :],
                                    start=True, stop=True), reads=['WsT', 'svn'], writes=[kmx])
        dve(lambda: V.tensor_tensor(out=t1[:], in0=pmx, in1=gsv[:], op=ALU.mult),
            reads=[kmx, 'gsv'], writes=['t1'])
        dve(lambda: V.tensor_tensor(out=t1[:].rearrange("p (a b) -> p a b", b=64),
                                    in0=t1[:].rearrange("p (a b) -> p a b", b=64),
                                    in1=bcol[:].unsqueeze(2).to_broadcast([128, 8, 64]), op=ALU.add),
            reads=['t1', 'bcol'], writes=['t1'])
        dve(lambda: V.tensor_tensor(out=gated[:], in0=t1[:], in1=psu, op=ALU.mult),
            reads=['t1', ksu], writes=['gated'])
        rms_rstd(gated[:], ['gated'], 512, 1)
        dve(lambda: V.scalar_tensor_tensor(out=gnb[:], in0=gated[:], scalar=rs[:, 1:2], in1=gso[:],
                                           op0=ALU.mult, op1=ALU.mult),
            reads=['gated', 'rs1', 'gso'], writes=['gnb'])
        dma('pool', Gs[tsl, :], gnb[:], reads=['gnb'], writes=['Gs'])

    def ab_load(i):
        dma('sp', xts[i % 2][:], x_d[i * 128:(i + 1) * 128, :], writes=['xt%d' % (i % 2)])

    ab_load(0)
    ab_load(1)
    ab_F1(0)
    ab_P(0)
    for i in range(NT):
        if i + 1 < NT:
            ab_F1(i + 1)
        if i + 2 < NT:
            ab_load(i + 2)
        ab_B1(i)
        ab_B2(i)
        cast_piece(i)
        if i + 1 < NT:
            ab_P(i + 1)

    dma('sp', bm, bm_d, writes=['RC', 'RCb'])
    Sbs = [sb("Sb%d" % i, [128, 2, 1024]) for i in range(2)]
    PTs = [sb("PTt%d" % i, [128, 2, 1024], BF16) for i in range(2)]
    vts = [sb("vt%d" % i, [128, 8, 65], BF16) for i in range(3)]
    Osb = [sb("Osb%d" % i, [128, 520]) for i in range(2)]
    iters = []
    for b, d in enumerate([1, 4, 16]):
        for r in range(d):
            for n in range(32 // d):
                iters.append((b, d, r, n))
    units = []
    for it, (b, d, r, n) in enumerate(iters):
        for w in ([0, 1] if n > 0 else [1]):
            units.append((it, w))

    def d_tok(it):
        b, d, r, n = iters[it]
        q0 = n * 128 * d + r
        return slice(q0, q0 + 127 * d + 1, d)

    def d_V(it):
        cur = it % 3
        dma('sp', vts[cur][:].rearrange("p h e -> p (h e)"), Vs[d_tok(it), :], reads=['Vs'], writes=['vt%d' % cur])

    def d_S(ui):
        it, w = units[ui]
        b, d, r, n = iters[it]
        qtok = d_tok(it)
        k0 = (n - 1 + w) * 128 * d + r
        ktok = slice(k0, k0 + 127 * d + 1, d)
        bb = 2 * (ui % 2)
        for par in range(2):
            for hp in range(4):
                o = bank(bb + par)[:, hp * 128:(hp + 1) * 128]
                pe(lambda o=o, par=par, hp=hp: T.matmul(
                    o, lhsT=qkT[par * 64:(par + 1) * 64, 4 + hp, ktok],
                    rhs=qkT[par * 64:(par + 1) * 64, hp, qtok], start=True, stop=True),
                   reads=['qkT'], writes=['b%d' % (bb + par)])

    def d_BE(ui):
        it, w = units[ui]
        b = iters[it][0]
        bb = 2 * (ui % 2)
        Sb, PTt = Sbs[it % 2], PTs[it % 2]
        ks, kp = 'Sb%d_%d' % (it % 2, w), 'PT%d_%d' % (it % 2, w)
        dve(lambda: V.scalar_tensor_tensor(
            out=Sb[:, w, :], in0=bank(bb, 2), scalar=0.125,
            in1=bm[:, (b * 2 + w) * 1024:(b * 2 + w + 1) * 1024], op0=ALU.mult, op1=ALU.add),
            reads=['b%d' % bb, 'b%d' % (bb + 1), 'RC'], writes=[ks])
        act(lambda: A.activation(out=PTt[:, w, :], in_=Sb[:, w, :], func=AF.Exp), reads=[ks], writes=[kp])

    def d_PV(it):
        b, d, r, n = iters[it]
        ws = [0, 1] if n > 0 else [1]
        cur, prv = it % 3, (it - 1) % 3
        PTt = PTs[it % 2]
        ob = 4 + 2 * (it % 2)
        for h in range(8):
            par, hp = h % 2, h // 2
            o = bank(ob + h // 4)[:, (h % 4) * 65:(h % 4) * 65 + 65]
            for wi, w in enumerate(ws):
                vt = vts[cur] if w == 1 else vts[prv]
                kvt = 'vt%d' % (cur if w == 1 else prv)
                col = (par * 4 + hp) * 128
                pe(lambda o=o, w=w, col=col, vt=vt, h=h, wi=wi: T.matmul(
                    o, lhsT=PTt[:, w, col:col + 128], rhs=vt[:, h, :],
                    start=(wi == 0), stop=(wi == len(ws) - 1)),
                   reads=['PT%d_%d' % (it % 2, w), kvt], writes=['b%d' % (ob + h // 4)])
        osb = Osb[it % 2]
        ko = 'Osb%d' % (it % 2)
        dve(lambda: V.tensor_copy(out=osb[:, 0:260], in_=bank(ob)[:, 0:260]), reads=['b%d' % ob], writes=[ko])
        dve(lambda: V.tensor_copy(out=osb[:, 260:520], in_=bank(ob + 1)[:, 0:260]), reads=['b%d' % (ob + 1)],
            writes=[ko])
        dma('pool', OB[b, d_tok(it), :], osb[:], reads=[ko], writes=['OB'])

    d_V(0)
    d_S(0)
    for ui, (it, w) in enumerate(units):
        if ui + 1 < len(units):
            nit, nw = units[ui + 1]
            if nit != it:
                d_V(nit)
            d_S(ui + 1)
        d_BE(ui)
        if ui + 1 == len(units) or units[ui + 1][0] != it:
            d_PV(it)

    S.barrier()
    dma('pool', Wo, w_out_d.rearrange("(c p) n -> p c n", p=128), writes=['RC'])
    dma('sp', gA[:], g2_d.partition_broadcast(128), writes=['gA'])
    obts, accs, rdens, attns, mixbs, mixTs = [], [], [], [], [], []
    for p in range(2):
        set_arena(A_WORK if p == 0 else (78848 + 16384, 119808))
        if p == 0:
            set_arena(A_WORK)
        obts.append(sb("obt%d" % p, [128, 3, 520]))
        accs.append(sb("acc%d" % p, [128, 520]))
        rdens.append(sb("rden%d" % p, [128, 8, 1]))
        attns.append(sb("attn%d" % p, [128, 512]))
        mixbs.append(sb("mixb%d" % p, [128, DM], BF16))
        mixTs.append(sb("mixT%d" % p, [128, 8, 128], BF16))
        if p == 0:
            x2 = [sb("x2_%d" % i, [128, DM]) for i in range(2)]
            h2b = sb("h2b", [128, DM], BF16)
            if stage >= 2:
                WqE = sb("WqE", [128, 8, 2048], BF16)
                h2t_sb = sb("h2t_sb", [128, 8, 128], BF16)
                qT_sb = sb("qT_sb", [128, 2048], BF16)
    if stage >= 2:
        dma('pool', WqE, w_q_d.rearrange("(c p) n -> p c n", p=128), writes=['WqE'])

    def e_E1(i):
        p = i % 2
        tsl = slice(i * 128, (i + 1) * 128)
        obt, acc, rden, attn, mixb, mixT = obts[p], accs[p], rdens[p], attns[p], mixbs[p], mixTs[p]
        ko, ka, kr, kt, kma, kms, kmt = ('obt%d' % p, 'acc%d' % p, 'rden%d' % p, 'attn%d' % p, 'mixb_a%d' % p,
                                         'mixb_s%d' % p, 'mixT%d' % p)
        acc3 = acc[:].rearrange("p (h e) -> p h e", e=65)
        dma('sp', obt[:], OB[:, tsl, :].rearrange("b t c -> t b c"), reads=['OB'], writes=[ko])
        dma('sp', mixb[:, 512:1024], Gs[tsl, :], reads=['Gs'], writes=[kms])
        dve(lambda: V.tensor_tensor(out=acc[:], in0=obt[:, 0, :], in1=obt[:, 1, :], op=ALU.add),
            reads=[ko], writes=[ka])
        dve(lambda: V.tensor_tensor(out=acc[:], in0=acc[:], in1=obt[:, 2, :], op=ALU.add),
            reads=[ko, ka], writes=[ka])
        dve(lambda: V.reciprocal(out=rden[:], in_=acc3[:, :, 64:65]), reads=[ka], writes=[kr])
        dve(lambda: V.tensor_tensor(out=attn[:].rearrange("p (h e) -> p h e", e=64), in0=acc3[:, :, 0:64],
                                    in1=rden[:].to_broadcast([128, 8, 64]), op=ALU.mult),
            reads=[ka, kr], writes=[kt])
        rms_rstd(attn[:], [kt], 512, 2)
        dve(lambda: V.scalar_tensor_tensor(out=mixb[:, 0:512], in0=attn[:], scalar=rs[:, 2:3], in1=gao[:],
                                           op0=ALU.mult, op1=ALU.mult),
            reads=[kt, 'rs2', 'gao'], writes=[kma])
        pT = bank_bf(0)
        for c in range(8):
            pe(lambda c=c: T.transpose(out=pT[:, c, :], in_=mixb[:, c * 128:(c + 1) * 128], identity=ident_b[:]),
               reads=[kma if c < 4 else kms, 'ident_b'], writes=['b0'])
        act(lambda: A.copy(out=mixT[:], in_=pT), reads=['b0'], writes=[kmt])

    def e_E2(i):
        p = i % 2
        tsl = slice(i * 128, (i + 1) * 128)
        mixT, kmt = mixTs[p], 'mixT%d' % p
        xt, kx = xts[p], 'xt%d' % p
        xx, k2 = x2[p], 'x2_%d' % p
        dma('sp', xt[:], x_d[tsl, :], writes=[kx])
        for half in range(2):
            for c in range(8):
                pe(lambda half=half, c=c: T.matmul(bank(1 + half), lhsT=mixT[:, c, :],
                                                   rhs=Wo[:, c, half * 512:(half + 1) * 512],
                                                   start=(c == 0), stop=(c == 7)),
                   reads=[kmt, 'RC'], writes=['b%d' % (1 + half)])
        dve(lambda: V.tensor_tensor(out=xx[:], in0=bank(1, 2), in1=xt[:], op=ALU.add),
            reads=['b1', 'b2', kx], writes=[k2])
        dma('pool', out_d[tsl, :], xx[:], reads=[k2], writes=['out%d' % i])
        if stage >= 2:
            rms_rstd(xx[:], [k2], DM, 3)
            dve(lambda: V.scalar_tensor_tensor(out=h2b[:], in0=xx[:], scalar=rs[:, 3:4], in1=gA[:],
                                               op0=ALU.mult, op1=ALU.mult),
                reads=[k2, 'rs3', 'gA'], writes=['h2b'])
            pT3 = bank_bf(3)
            for c in range(8):
                pe(lambda c=c: T.transpose(out=pT3[:, c, :], in_=h2b[:, c * 128:(c + 1) * 128], identity=ident_b[:]),
                   reads=['h2b', 'ident_b'], writes=['b3'])
            act(lambda: A.copy(out=h2t_sb[:], in_=pT3), reads=['b3'], writes=['h2t_sb'])
            dma('pool', H2S[i], h2t_sb[:].rearrange("p a b -> p (a b)"), reads=['h2t_sb'], writes=['H2S%d' % i])

    def e_E3(i):
        if stage < 2:
            return
        for j in range(16):
            for c in range(8):
                pe(lambda j=j, c=c: T.matmul(bank(4 + j // 4)[:, (j % 4) * 128:(j % 4 + 1) * 128],
                                             lhsT=WqE[:, c, j * 128:(j + 1) * 128], rhs=h2t_sb[:, c, :],
                                             start=(c == 0), stop=(c == 7)),
                   reads=['WqE', 'h2t_sb'], writes=['b%d' % (4 + j // 4)])
        act(lambda: A.copy(out=qT_sb[:], in_=bank(4, 4)), reads=['b4', 'b5', 'b6', 'b7'], writes=['qT_sb'])
        dma('pool', QS[i], qT_sb[:], reads=['qT_sb'], writes=['QS%d' % i])

    e_E1(0)
    for i in range(NT):
        e_E2(i)
        if i + 1 < NT:
            e_E1(i + 1)
        e_E3(i)
        cast_piece(32 + i)

    if stage >= 2:
        S.barrier()
        peer(nc, S, locals())

    S.final_wait('sp')
    S.emit()
    return nc


def peer(nc, S, L):
    V, A, T, G, SPQ = nc.vector, nc.scalar, nc.tensor, nc.gpsimd, nc.sync
    sb, bank, dma, dve, act, pe, pool = L['sb'], L['bank'], L['dma'], L['dve'], L['act'], L['pe'], L['pool']
    keysT = L['keysT']
    ident_f, iota_f, c16 = L['ident_f'], L['iota_f'], L['c16']
    out_d, UTb, EVb, stage, H2S, QS = L['out_d'], L['UTb'], L['EVb'], L['stage'], L['H2S'], L['QS']
    xts = L['xts']
    set_arena = L['set_arena']

    set_arena((13312, 78848))
    GT = sb("GT", [128, TG, 128], BF16)
    set_arena((78848, 208000))
    e1T = sb("e1T", [128, SEQ], BF16)
    e2T = sb("e2T", [128, SEQ], BF16)
    gTt = sb("gTt", [128, SEQ], F32)
    qTs = sb("qTs", [128, 16, 128], BF16)
    ssb = sb("ssb", [128, 16, 128])
    tmp = [sb("tmp%d" % i, [128, 256]) for i in range(2)]
    v16 = sb("v16", [128, 16, 16])
    idx = sb("idx", [128, 16, 16], U32)
    idxf = sb("idxf", [128, 16, 16])
    gq = sb("gq", [128, 4096], BF16)
    cand = gq.bitcast(F32).rearrange("p (h c) -> p h c", h=8)
    CK = ['cand', 'ge4', 'eq4']
    best = sb("best", [128, 8, 16])
    pos = sb("pos", [128, 8, 16], U32)
    posf = sb("posf", [128, 8, 16])
    eb = sb("eb", [128, 8, 16])
    esum = sb("esum", [128, 8])
    rsum = sb("rsum", [128, 8])
    gts = [sb("gt%d" % i, [128, 128]) for i in range(2)]
    ge4 = gq[:, 0:2048].rearrange("p (a b c) -> p a b c", a=8, b=16)
    eq4 = gq[:, 2048:4096].rearrange("p (a b c) -> p a b c", a=8, b=16)
    r1raw = sb("r1raw", [128, 8, 16])
    r2t = sb("r2t", [128, 8, 16])
    e1s = [sb("e1_%d" % i, [128, 128]) for i in range(2)]
    e2s = [sb("e2_%d" % i, [128, 128]) for i in range(2)]
    thr16 = c16[:, 0:16]
    io16p1 = c16[:, 16:32]
    io16m16 = c16[:, 32:48]
    v4 = v16[:].rearrange("p (h two) r -> p h two r", two=2)
    i4 = idxf[:].rearrange("p (h two) r -> p h two r", two=2)
    cand4 = cand[:].rearrange("p h (a b) -> p h a b", b=16)

    def F_tile(i):
        tsl = slice(i * 128, (i + 1) * 128)
        e1, e2, gt = e1s[i % 2], e2s[i % 2], gts[i % 2]
        ke1, ke2, kgt = 'e1_%d' % (i % 2), 'e2_%d' % (i % 2), 'gt%d' % (i % 2)
        dma('sp', qTs[:].rearrange("p a b -> p (a b)"), QS[i], reads=['QS%d' % i], writes=['qTs'])
        yield None
        for qd in range(4):
            for jj in range(4):
                j = qd * 4 + jj
                pe(lambda j=j, jj=jj: T.matmul(bank(7)[:, jj * 128:(jj + 1) * 128], lhsT=qTs[:, j, :],
                                               rhs=keysT[:, j % 2, :], start=True, stop=True),
                   reads=['qTs', 'keysT'], writes=['b7'])
            act(lambda qd=qd: A.copy(out=ssb[:, qd * 4:(qd + 1) * 4, :].rearrange("p a b -> p (a b)"), in_=bank(7)),
                reads=['b7'], writes=['ssb%d' % qd])
            yield None
        for j in range(16):
            tm = tmp[j % 2]
            kt = 'tmp%d' % (j % 2)
            ks = 'ssb%d' % (j // 4)
            dve(lambda j=j: V.max(out=v16[:, j, 0:8], in_=ssb[:, j, :]), reads=[ks], writes=['v16'])
            dve(lambda j=j, tm=tm: V.match_replace(out=tm[:, 0:128], in_to_replace=v16[:, j, 0:8],
                                                   in_values=ssb[:, j, :], imm_value=-1e30),
                reads=[ks, 'v16'], writes=[kt])
            dve(lambda j=j, tm=tm: V.max(out=v16[:, j, 8:16], in_=tm[:, 0:128]), reads=[kt], writes=['v16'])
            dve(lambda j=j: V.max_index(out=idx[:, j, 0:8], in_max=v16[:, j, 0:8], in_values=ssb[:, j, :]),
                reads=[ks, 'v16'], writes=['idx'])
            dve(lambda j=j: V.max_index(out=idx[:, j, 8:16], in_max=v16[:, j, 8:16], in_values=ssb[:, j, :]),
                reads=[ks, 'v16'], writes=['idx'])
            if j % 2 == 1:
                yield None
        dve(lambda: V.tensor_copy(out=idxf[:], in_=idx[:]), reads=['idx'], writes=['idxf'])
        dve(lambda: V.tensor_tensor(out=cand4, in0=v4[:, :, 0, :].unsqueeze(3).to_broadcast([128, 8, 16, 16]),
                                    in1=v4[:, :, 1, :].unsqueeze(2).to_broadcast([128, 8, 16, 16]), op=ALU.add),
            reads=['v16'], writes=CK)
        yield None
        for h in range(8):
            tm = tmp[h % 2]
            kt = 'tmp%d' % (h % 2)
            dve(lambda h=h: V.max(out=best[:, h, 0:8], in_=cand[:, h, :]), reads=CK, writes=['best'])
            dve(lambda h=h, tm=tm: V.match_replace(out=tm[:], in_to_replace=best[:, h, 0:8], in_values=cand[:, h, :],
                                                   imm_value=-1e30), reads=CK + ['best'], writes=[kt])
            dve(lambda h=h, tm=tm: V.max(out=best[:, h, 8:16], in_=tm[:]), reads=[kt], writes=['best'])
            dve(lambda h=h: V.max_index(out=pos[:, h, 0:8], in_max=best[:, h, 0:8], in_values=cand[:, h, :]),
                reads=CK + ['best'], writes=['pos'])
            dve(lambda h=h: V.max_index(out=pos[:, h, 8:16], in_max=best[:, h, 8:16], in_values=cand[:, h, :]),
                reads=CK + ['best'], writes=['pos'])
            if h % 2 == 1:
                yield None
        dve(lambda: V.tensor_tensor(out=eb[:], in0=best[:], in1=best[:, :, 0:1].to_broadcast([128, 8, 16]),
                                    op=ALU.subtract), reads=['best'], writes=['eb'])
        act(lambda: A.activation(out=eb[:], in_=eb[:], func=AF.Exp), reads=['eb'], writes=['eb'])
        dve(lambda: V.tensor_reduce(out=esum[:], in_=eb[:], axis=AX.X, op=ALU.add), reads=['eb'], writes=['esum'])
        dve(lambda: V.reciprocal(out=rsum[:], in_=esum[:]), reads=['esum'], writes=['rsum'])
        dve(lambda: V.tensor_tensor(out=gt[:].rearrange("p (h k) -> p h k", k=16), in0=eb[:],
                                    in1=rsum[:].unsqueeze(2).to_broadcast([128, 8, 16]), op=ALU.mult),
            reads=['eb', 'rsum'], writes=[kgt])
        yield None
        dve(lambda: V.tensor_copy(out=posf[:], in_=pos[:]), reads=['pos'], writes=['posf'])
        dve(lambda: V.tensor_tensor(out=ge4[:], in0=posf[:].unsqueeze(3).to_broadcast([128, 8, 16, 16]),
                                    in1=thr16[:, None, None, :].to_broadcast([128, 8, 16, 16]), op=ALU.is_ge),
            reads=['posf', 'c16'], writes=['ge4'])
        dve(lambda: V.tensor_reduce(out=r1raw[:], in_=ge4[:], axis=AX.X, op=ALU.add), reads=['ge4'], writes=['r1raw'])
        dve(lambda: V.tensor_tensor(out=eq4[:], in0=r1raw[:].unsqueeze(3).to_broadcast([128, 8, 16, 16]),
                                    in1=io16p1[:, None, None, :].to_broadcast([128, 8, 16, 16]), op=ALU.is_equal),
            reads=['r1raw', 'c16'], writes=['eq4'])
        dve(lambda: V.tensor_tensor(out=eq4[:], in0=eq4[:],
                                    in1=i4[:, :, 0, :].unsqueeze(2).to_broadcast([128, 8, 16, 16]), op=ALU.mult),
            reads=['eq4', 'idxf'], writes=['eq4'])
        dve(lambda: V.tensor_reduce(out=e1[:].rearrange("p (h k) -> p h k", k=16), in_=eq4[:], axis=AX.X, op=ALU.add),
            reads=['eq4'], writes=[ke1])
        dve(lambda: V.scalar_tensor_tensor(out=r2t[:], in0=r1raw[:], scalar=-16.0, in1=posf[:],
                                           op0=ALU.mult, op1=ALU.add), reads=['r1raw', 'posf'], writes=['r2t'])
        dve(lambda: V.tensor_tensor(out=ge4[:], in0=r2t[:].unsqueeze(3).to_broadcast([128, 8, 16, 16]),
                                    in1=io16m16[:, None, None, :].to_broadcast([128, 8, 16, 16]), op=ALU.is_equal),
            reads=['r2t', 'c16'], writes=['ge4'])
        dve(lambda: V.tensor_tensor(out=ge4[:], in0=ge4[:],
                                    in1=i4[:, :, 1, :].unsqueeze(2).to_broadcast([128, 8, 16, 16]), op=ALU.mult),
            reads=['ge4', 'idxf'], writes=['ge4'])
        dve(lambda: V.tensor_reduce(out=e2[:].rearrange("p (h k) -> p h k", k=16), in_=ge4[:], axis=AX.X, op=ALU.add),
            reads=['ge4'], writes=[ke2])
        yield 'tail'
        for m, (src, ks) in enumerate([(e1, ke1), (e2, ke2), (gt, kgt)]):
            pe(lambda m=m, src=src: T.transpose(out=bank(7)[:, m * 128:(m + 1) * 128], in_=src[:], identity=ident_f[:]),
               reads=[ks, 'ident_f'], writes=['b7'])
        act(lambda tsl=tsl: A.copy(out=e1T[:, tsl], in_=bank(7)[:, 0:128]), reads=['b7'], writes=['e1T%d' % i])
        act(lambda tsl=tsl: A.copy(out=e2T[:, tsl], in_=bank(7)[:, 128:256]), reads=['b7'], writes=['e2T%d' % i])
        act(lambda tsl=tsl: A.copy(out=gTt[:, tsl], in_=bank(7)[:, 256:384]), reads=['b7'], writes=['gT%d' % i])

    def exhaust(gen):
        for _ in gen:
            pass

    if stage < 3:
        for i in range(NT):
            exhaust(F_tile(i))
        return

    SBK = 16
    Xs = [sb("X%d" % i, [128, SBK, 128], BF16) for i in range(2)]
    Yts = [sb("Yt%d" % i, [128, SBK, 128], BF16) for i in range(2)]
    Ys = [sb("Y%d" % i, [128, SBK, 128], BF16) for i in range(2)]
    NB = 6
    uts = [sb("ut%d" % i, [128, 8, 128], BF16) for i in range(NB)]
    vcs = [sb("vc%d" % i, [128, DM], BF16) for i in range(NB)]
    ges = [sb("ge%d" % i, [128, TG], BF16) for i in range(3)]
    cts = [sb("ct%d" % i, [128, TG], BF16) for i in range(3)]
    res = Yts[1].rearrange("p a b -> p (a b)").bitcast(F32)
    h2g = [sb("h2g%d" % i, [128, 8, TG], BF16) for i in range(2)]
    NG = SEQ // TG
    ROWS = 16384 // 64
    blk = 0
    gstate = {'gcnt': 0}

    def load_h2g(g):
        for tt in range(2):
            dma('sp', h2g[g % 2][:, :, tt * 128:(tt + 1) * 128],
                H2S[2 * g + tt].rearrange("p (a b) -> p a b", a=8),
                reads=['H2S%d' % (2 * g + tt)], writes=['h2g%d' % (g % 2)])

    exhaust(F_tile(0))
    exhaust(F_tile(1))
    load_h2g(0)
    for g in range(NG):
        t0 = g * TG
        nblk_g = TG // SBK

        def gb_s1(s_):
            bi = blk + s_
            ts0 = t0 + s_ * SBK
            ti = ts0 // 128
            X, Yt = Xs[bi % 2], Yts[bi % 2]
            dve(lambda: V.tensor_tensor(
                out=X[:], in0=iota_f[:, None, :].to_broadcast([128, SBK, 128]),
                in1=e1T[:, ts0:ts0 + SBK].unsqueeze(2).to_broadcast([128, SBK, 128]), op=ALU.is_equal),
                reads=['iota_f', 'e1T%d' % ti], writes=['X%d' % (bi % 2)])
            dve(lambda: V.tensor_tensor(
                out=Yt[:], in0=iota_f[:, None, :].to_broadcast([128, SBK, 128]),
                in1=e2T[:, ts0:ts0 + SBK].unsqueeze(2).to_broadcast([128, SBK, 128]), op=ALU.is_equal),
                reads=['iota_f', 'e2T%d' % ti], writes=['Yt%d' % (bi % 2)])

        def gb_s23(s_):
            nonlocal_gcnt = gstate['gcnt']
            bi = blk + s_
            ts0 = t0 + s_ * SBK
            ti = ts0 // 128
            X, Yt, Y = Xs[bi % 2], Yts[bi % 2], Ys[bi % 2]
            kX, kYt = 'X%d' % (bi % 2), 'Yt%d' % (bi % 2)
            for tt in range(10):
                act(lambda tt=tt: A.activation(out=Y[:, tt, :], in_=Yt[:, tt, :], func=AF.Copy,
                                               scale=gTt[:, ts0 + tt:ts0 + tt + 1]),
                    reads=[kYt, 'gT%d' % ti], writes=['Y%d_%d' % (bi % 2, tt)])
            dve(lambda: V.tensor_tensor(out=Y[:, 10:16, :], in0=Yt[:, 10:16, :],
                                        in1=gTt[:, ts0 + 10:ts0 + 16].unsqueeze(2).to_broadcast([128, 6, 128]),
                                        op=ALU.mult),
                reads=[kYt, 'gT%d' % ti], writes=['Y%d_%d' % (bi % 2, tt) for tt in range(10, 16)])
            for oct_ in range(SBK // 8):
                gc = gstate['gcnt']
                pb = 4 + 2 * (gc % 2)
                for u in range(8):
                    tt = oct_ * 8 + u
                    o = bank(pb + u // 4)[:, (u % 4) * 128:(u % 4 + 1) * 128]
                    pe(lambda o=o, tt=tt: T.matmul(o, lhsT=Y[:, tt, :], rhs=X[:, tt, :], start=True, stop=True),
                       reads=[kX, 'Y%d_%d' % (bi % 2, tt)], writes=['b%d' % (pb + u // 4)])
                tl = s_ * SBK + oct_ * 8
                if True:
                    act(lambda pb=pb, tl=tl: A.copy(out=GT[:, tl:tl + 8, :].rearrange("p t i -> p (t i)"),
                                                    in_=bank(pb, 2)),
                        reads=['b%d' % pb, 'b%d' % (pb + 1)], writes=['GT'])
                else:
                    dve(lambda pb=pb, tl=tl: V.tensor_copy(out=GT[:, tl:tl + 8, :].rearrange("p t i -> p (t i)"),
                                                          in_=bank(pb, 2)),
                        reads=['b%d' % pb, 'b%d' % (pb + 1)], writes=['GT'])
                gstate['gcnt'] += 1

        if not gstate.get('pre_s1'):
            gb_s1(0)
        gstate['pre_s1'] = False
        for s_ in range(nblk_g):
            if s_ + 1 < nblk_g:
                gb_s1(s_ + 1)
            gb_s23(s_)
        blk += nblk_g

        hg = h2g[g % 2]
        khg = 'h2g%d' % (g % 2)

        def load_chunk(c):
            ut, vc = uts[c % NB], vcs[c % NB]
            dma('sp', ut[:].rearrange("p a b -> p (a b)"), UTb[c * 128:(c + 1) * 128, :],
                reads=['UTb%d' % (c * 128 // ROWS)], writes=['ut%d' % (c % NB)])
            dma('sp', vc[:], EVb[c * 128:(c + 1) * 128, :],
                reads=['EVb%d' % (c * 128 // ROWS)], writes=['vc%d' % (c % NB)])

        def a_mm(c, hg=hg, khg=khg):
            ut = uts[c % NB]
            pa = bank(4 + c % 3)[:, 0:TG]
            for dc in range(8):
                pe(lambda ut=ut, pa=pa, dc=dc, hg=hg: T.matmul(pa, lhsT=ut[:, dc, :], rhs=hg[:, dc, :],
                                                               start=(dc == 0), stop=(dc == 7)),
                   reads=['ut%d' % (c % NB), khg], writes=['b%d' % (4 + c % 3)])

        gens = [F_tile(2 * g + 2), F_tile(2 * g + 3)] if g + 1 < NG else []
        tails = []
        state = {'cur': 0}

        def step():
            while state['cur'] < len(gens):
                try:
                    r = next(gens[state['cur']])
                except StopIteration:
                    state['cur'] += 1
                    continue
                if r == 'tail':
                    tails.append(gens[state['cur']])
                    state['cur'] += 1
                    continue
                return

        for c0 in range(5):
            load_chunk(c0)
        a_mm(0)
        a_mm(1)
        for c in range(128):
            if c + 5 < 128:
                load_chunk(c + 5)
            if c + 2 < 128:
                a_mm(c + 2)
            gel, ct, vc = ges[c % 3], cts[c % 3], vcs[c % NB]
            act(lambda gel=gel, c=c: A.activation(out=gel[:], in_=bank(4 + c % 3)[:, 0:TG], func=AF.Gelu),
                reads=['b%d' % (4 + c % 3)], writes=['ge%d' % (c % 3)])
            pool(lambda gel=gel, ct=ct, c=c: G.tensor_tensor(out=ct[:], in0=gel[:], in1=GT[:, :, c], op=ALU.mult),
                 reads=['ge%d' % (c % 3), 'GT'], writes=['ct%d' % (c % 3)])
            for tt in range(2):
                for dh in range(2):
                    pe(lambda ct=ct, vc=vc, tt=tt, dh=dh, c=c: T.matmul(
                        bank(tt * 2 + dh), lhsT=ct[:, tt * 128:(tt + 1) * 128], rhs=vc[:, dh * 512:(dh + 1) * 512],
                        start=(c == 0), stop=(c == 127)),
                       reads=['ct%d' % (c % 3), 'vc%d' % (c % NB)], writes=['b%d' % (tt * 2 + dh)])
            if c >= 4 and c % 2 == 0 and c < 120:
                step()
            if c == 64 and g + 1 < NG:
                load_h2g(g + 1)
            if c == 120:
                while state['cur'] < len(gens):
                    step()
                for tg_ in tails:
                    exhaust(tg_)
        if g + 1 < NG:
            t0 = (g + 1) * TG
            gb_s1(0)
            gstate['pre_s1'] = True
        for tt in range(2):
            i = 2 * g + tt
            tsl = slice(i * 128, (i + 1) * 128)
            xt = xts[tt]
            kx = 'xt%d' % tt
            dma('sp', xt[:], out_d[tsl, :], reads=['out%d' % i], writes=[kx])
            dve(lambda xt=xt, tt=tt: V.tensor_tensor(out=res[:], in0=bank(tt * 2, 2), in1=xt[:], op=ALU.add),
                reads=['b%d' % (tt * 2), 'b%d' % (tt * 2 + 1), kx], writes=['Yt1'])
            dma('sp', out_d[tsl, :], res[:], reads=['Yt1'], writes=['out%d' % i])


_CACHE = {}


def host_consts():
    k = np.arange(128)[:, None]
    q = np.arange(128)[None, :]
    bm = np.zeros((128, 3, 2, 2, 4, 128), np.float32)
    for b, d in enumerate([1, 4, 16]):
        for w in range(2):
            dist = (q - k) if w == 1 else (128 + q - k)
            valid = (dist >= 0) if w == 1 else (dist <= 128)
            for h in range(8):
                slope = 2.0 ** (-(h + 1))
                val = np.where(valid, -slope * d * dist, NEG).astype(np.float32)
                bm[:, b, w, h % 2, h // 2, :] = val
    ident = np.eye(128, dtype=np.float32)
    iota = np.broadcast_to(np.arange(128, dtype=np.float32)[None, :], (128, 128)).copy()
    r = np.arange(16, dtype=np.float32)
    c16 = np.broadcast_to(np.concatenate([16 * r, r + 1, r - 16])[None, :], (128, 48)).copy()
    s = np.arange(128)[:, None]
    t = np.arange(128)[None, :]
    trilT = (t >= s).astype(np.float32)
    return dict(bm=bm.reshape(128, 6144), ident=ident, iota=iota, c16=c16, trilT=trilT)


def make_in_maps(inputs, stage=9):
    f = lambda a: np.ascontiguousarray(np.asarray(a, dtype=np.float32))
    x = f(inputs['x'])
    cs = host_consts()
    eu = f(inputs['expert_u'])[0]
    uT = np.ascontiguousarray(eu.reshape(128, 128, 8, 128).transpose(0, 3, 2, 1)).reshape(128 * 128, DM)
    shared = dict(
        w_in=f(inputs['w_in'])[0], w_out=f(inputs['w_out'])[0], w_query=f(inputs['w_query'])[0],
        g1=f(inputs['norm1_g']).reshape(1, DM), g2=f(inputs['norm2_g']).reshape(1, DM),
        gqk=np.concatenate([np.tile(f(inputs['q_norm_g'])[0], 8), np.tile(f(inputs['k_norm_g'])[0], 8)]).reshape(1, DM),
        gsv=f(inputs['sgu_norm_g']).reshape(1, 512), gao=f(inputs['attn_out_g']).reshape(1, 512),
        gso=f(inputs['sgu_out_g']).reshape(1, 512),
        wsT=np.ascontiguousarray(f(inputs['sgu_w'])[0].transpose(2, 0, 1)),
        bcol=np.ascontiguousarray(f(inputs['sgu_b'])[0].T),
        keysT=np.ascontiguousarray(f(inputs['sub_keys'])[0].transpose(2, 0, 1)),
        uT=uT, ev=f(inputs['expert_v'])[0], **cs)
    return [dict(x=np.ascontiguousarray(x[b]), **shared) for b in range(x.shape[0])]


def kernel(**inputs):
    if 'nc' not in _CACHE:
        _CACHE['nc'] = build_program()
    nc = _CACHE['nc']
    in_maps = make_in_maps(inputs)
    n = len(in_maps)
    res = run_bass_kernel_spmd(nc, in_maps, core_ids=list(range(n)))
    return np.stack([np.asarray(r["out"], dtype=np.float32) for r in res.results], axis=0)
```

```python
import contextlib
import numpy as np
import concourse.bass as bass
import concourse.mybir as mybir
from concourse.bass_utils import run_bass_kernel_spmd

F32 = mybir.dt.float32
BF16 = mybir.dt.bfloat16
U32 = mybir.dt.uint32
AF = mybir.ActivationFunctionType
ALU = mybir.AluOpType
AX = mybir.AxisListType

ENGS = ['pe', 'act', 'dve', 'pool', 'sp']
EPOCH = 30000

SEQ = 4096
DM = 1024
NT = SEQ // 128
EPS = 1e-6
TG = 256
NEG = -30000.0


class Sched:
    NS = 16

    def __init__(self, nc):
        self.nc = nc
        self.q = {e: [] for e in ENGS}
        self.cnt = {e: 0 for e in ENGS}
        self.seen = {e: {} for e in ENGS}
        self.lastw = {}
        self.readers = {}
        self.sems = {}
        self.dma_n = {e: 0 for e in ENGS}
        self.dma_slot_val = {}
        self.dma_slot_tok = {}

    def op(self, eng, fn, reads=(), writes=(), dma=False):
        deps = {}

        def add(tok):
            if tok is None:
                return
            k, v = tok
            if deps.get(k, 0) < v:
                deps[k] = v
        for k in reads:
            add(self.lastw.get(k))
        for k in writes:
            add(self.lastw.get(k))
            for t in self.readers.get(k, ()):
                add(t)
        if dma:
            slot = self.dma_n[eng] % self.NS
            self.dma_n[eng] += 1
            skey = "d_%s_%d" % (eng, slot)
            add(self.dma_slot_tok.get(skey))
            val = self.dma_slot_val.get(skey, 0) + 16
            self.dma_slot_val[skey] = val
            tok = (skey, val)
            self.dma_slot_tok[skey] = tok
            inc = 16
        else:
            c = self.cnt[eng]
            self.cnt[eng] += 1
            skey = "c_%s_%d" % (eng, c // EPOCH)
            tok = (skey, c % EPOCH + 1)
            inc = 1
        waits = []
        for k, v in deps.items():
            if eng == 'pe' and k.startswith('c_pe'):
                continue
            if self.seen[eng].get(k, 0) < v:
                self.seen[eng][k] = v
                waits.append((k, v))
        self.q[eng].append((waits, fn, skey, inc))
        for k in reads:
            self.readers.setdefault(k, []).append(tok)
        for k in writes:
            self.lastw[k] = tok
            self.readers[k] = []
        return tok

    def barrier(self):
        toks = list(self.dma_slot_tok.values())
        for e in ENGS:
            c = self.cnt[e]
            if c > 0:
                toks.append(("c_%s_%d" % (e, (c - 1) // EPOCH), (c - 1) % EPOCH + 1))
        for e in ENGS:
            waits = []
            for k, v in toks:
                if self.seen[e].get(k, 0) < v:
                    self.seen[e][k] = v
                    waits.append((k, v))
            if waits:
                self.q[e].append((waits, None, None, 0))

    def final_wait(self, eng):
        waits = list(self.dma_slot_tok.values())
        self.q[eng].append((waits, None, None, 0))

    def emit(self):
        nc = self.nc
        engmap = {'pe': 'tensor', 'act': 'scalar', 'dve': 'vector', 'pool': 'gpsimd', 'sp': 'sync'}
        stack = contextlib.ExitStack()
        keys = []
        for e in ENGS:
            for (waits, fn, skey, inc) in self.q[e]:
                for k, v in waits:
                    if k not in keys:
                        keys.append(k)
                if skey is not None and skey not in keys:
                    keys.append(skey)
        for k in keys:
            self.sems[k] = stack.enter_context(nc.semaphore("s_" + k))
        with stack, nc.Block() as block:
            for e in ENGS:
                items = self.q[e]

                def body(engine, items=items):
                    for (waits, fn, skey, inc) in items:
                        for k, v in waits:
                            engine.wait_ge(self.sems[k], v)
                        if fn is not None:
                            fn().then_inc(self.sems[skey], inc)
                getattr(block, engmap[e])(body)


def build_program(stage=9, dbg=False):
    nc = bass.Bass("TRN2", target_bir_lowering=False)
    S = Sched(nc)
    V, A, T, G, SPQ = nc.vector, nc.scalar, nc.tensor, nc.gpsimd, nc.sync

    def din(name, shape, dt=F32):
        return nc.dram_tensor(name, shape, dt, kind="ExternalInput").ap()

    x_d = din("x", [SEQ, DM])
    w_in_d = din("w_in", [DM, 2560])
    w_out_d = din("w_out", [DM, DM])
    w_q_d = din("w_query", [DM, 2048])
    g1_d = din("g1", [1, DM])
    g2_d = din("g2", [1, DM])
    gqk_d = din("gqk", [1, DM])
    gsv_d = din("gsv", [1, 512])
    gao_d = din("gao", [1, 512])
    gso_d = din("gso", [1, 512])
    wsT_d = din("wsT", [128, 8, 128])
    tril_d = din("trilT", [128, 128])
    bcol_d = din("bcol", [128, 8])
    keysT_d = din("keysT", [128, 2, 128])
    uT_d = din("uT", [128 * 128, DM])
    ev_d = din("ev", [16384, DM])
    bm_d = din("bm", [128, 3 * 2048])
    ident_d = din("ident", [128, 128])
    iota_d = din("iota", [128, 128])
    c16_d = din("c16", [128, 48])
    out_d = nc.dram_tensor("out", [SEQ, DM], F32, kind="ExternalOutput").ap()

    Vs = nc.dram_tensor("Vs", [SEQ, 520], BF16).ap()
    Gs = nc.dram_tensor("Gs", [SEQ, 512], BF16).ap()
    OB = nc.dram_tensor("OB", [3, SEQ, 520], F32).ap()
    UTb = nc.dram_tensor("UTb", [128 * 128, DM], BF16).ap()
    EVb = nc.dram_tensor("EVb", [16384, DM], BF16).ap()
    H2S = nc.dram_tensor("H2S", [NT, 128, DM], BF16).ap()
    QS = nc.dram_tensor("QS", [NT, 128, 2048], BF16).ap()

    BIG = nc.alloc_sbuf_tensor("BIG", [128, 104000], BF16)
    arena = {'segs': None}

    def set_arena(*segs):
        arena['segs'] = [list(sg) for sg in segs]

    def sb(name, shape, dt=F32):
        esz = 2 if dt == BF16 else 4
        n = 1
        for d_ in shape[1:]:
            n *= d_
        nbytes = (n * esz + 31) // 32 * 32
        for sg in arena['segs']:
            if sg[0] + nbytes <= sg[1]:
                off = sg[0]
                sg[0] += nbytes
                break
        else:
            raise RuntimeError("arena full for %s %s" % (name, shape))
        ap = BIG[:, off // 2: off // 2 + n * esz // 2]
        if dt != BF16:
            ap = ap.bitcast(dt)
        if len(shape) == 3:
            ap = ap.rearrange("p (a b) -> p a b", a=shape[1])
        elif len(shape) == 4:
            ap = ap.rearrange("p (a b c) -> p a b c", a=shape[1], b=shape[2])
        return ap

    A_GLOB = (0, 13312)
    A_RA = (13312, 78848)
    A_RC = (78848, 119808)
    A_MIX = (119808, 141312)
    A_WORK = (141312, 208000)
    A_GT = (78848, 144384)
    A_ET = (144384, 168960)
    A_FG = (168960, 208000)
    A_F2 = (119808, 144384)

    PS = nc.alloc_psum_tensor("PS", [128, 4096], F32)

    def bank(b, n=1):
        return PS[:, b * 512:(b + n) * 512]

    def bank_bf(b):
        return bank(b).bitcast(BF16).rearrange("p (a b) -> p a b", a=8)

    def dma(eng, out, in_, reads=(), writes=()):
        q = SPQ if eng == 'sp' else G
        return S.op(eng, lambda: q.dma_start(out=out, in_=in_), reads, writes, dma=True)

    def dve(fn, reads=(), writes=()):
        return S.op('dve', fn, reads, writes)

    def act(fn, reads=(), writes=()):
        return S.op('act', fn, reads, writes)

    def pe(fn, reads=(), writes=()):
        return S.op('pe', fn, reads, writes)

    def pool(fn, reads=(), writes=()):
        return S.op('pool', fn, reads, writes)

    set_arena(A_RA)
    RA = sb("RA", [128, 8, SEQ], BF16)
    qkT = RA
    set_arena(A_RC)
    RC = sb("RC", [128, 20480], BF16)
    W1 = RC.rearrange("p (c n) -> p c n", c=8)
    bm = RC[:, 0:12288].bitcast(F32)
    Wo = RC[:, 0:8192].rearrange("p (c n) -> p c n", c=8)
    Wq = RC[:, 0:16384].rearrange("p (c n) -> p c n", c=8)

    set_arena(A_GLOB)
    ident_f = sb("ident_f", [128, 128])
    ident_b = sb("ident_b", [128, 128], BF16)
    iota_f = sb("iota_f", [128, 128])
    c16 = sb("c16", [128, 48])
    keysT = sb("keysT", [128, 2, 128], BF16)
    eps_t = sb("eps_t", [128, 1])
    ss = sb("ss", [128, 4])
    sd = sb("sd", [128, 4])
    rs = sb("rs", [128, 4])
    ss16 = sb("ss16", [128, 16])
    sd16 = sb("sd16", [128, 16])
    r16 = sb("r16", [128, 16])
    junk = sb("junk", [128, DM], BF16)
    xts = [sb("xt%d" % i, [128, DM]) for i in range(2)]
    set_arena(A_MIX)
    gA = sb("gA", [128, DM])
    gqk = sb("gqk", [128, DM])
    gsv = sb("gsv", [128, 512])
    gao = sb("gao", [128, 512])
    gso = sb("gso", [128, 512])
    wsT_f = sb("wsT_f", [128, 8, 128])
    tril = sb("tril", [128, 128])
    WsT = sb("WsT", [128, 8, 128], BF16)
    bcol = sb("bcol", [128, 8])

    dma('sp', ident_f[:], ident_d, writes=['ident_f'])
    dma('pool', ident_b[:], ident_d, writes=['ident_b'])
    w_in_v = w_in_d.rearrange("(c p) n -> p c n", p=128)
    dma('pool', W1[:, :, 0:1280], w_in_v[:, :, 0:1280], writes=['RC'])
    dma('pool', W1[:, :, 1280:2560], w_in_v[:, :, 1280:2560], reads=['RC'], writes=['RCb'])
    dma('sp', iota_f[:], iota_d, writes=['iota_f'])
    dma('sp', c16[:], c16_d, writes=['c16'])
    dma('sp', gA[:], g1_d.partition_broadcast(128), writes=['gA'])
    dma('sp', gqk[:], gqk_d.partition_broadcast(128), writes=['gqk'])
    dma('sp', gsv[:], gsv_d.partition_broadcast(128), writes=['gsv'])
    dma('sp', gao[:], gao_d.partition_broadcast(128), writes=['gao'])
    dma('sp', gso[:], gso_d.partition_broadcast(128), writes=['gso'])
    dma('sp', wsT_f[:], wsT_d, writes=['wsT_f'])
    dma('sp', tril[:], tril_d, writes=['tril'])
    dma('sp', bcol[:], bcol_d, writes=['bcol'])
    dma('pool', keysT[:], keysT_d, writes=['keysT'])
    dve(lambda: V.memset(eps_t[:], EPS), writes=['eps'])
    dve(lambda: V.tensor_tensor(out=WsT[:], in0=wsT_f[:], in1=tril[:, None, :].to_broadcast([128, 8, 128]),
                                op=ALU.mult), reads=['wsT_f', 'tril'], writes=['WsT'])
    NPIECE = 64
    PROWS = 16384 // NPIECE

    def cast_piece(k):
        if stage < 3:
            return
        rs_ = slice(k * PROWS, (k + 1) * PROWS)
        dma('pool', UTb[rs_, :], uT_d[rs_, :], writes=['UTb%d' % k])
        dma('pool', EVb[rs_, :], ev_d[rs_, :], writes=['EVb%d' % k])

    def rms_rstd(src_ap, src_keys, n, col):
        act(lambda: A.activation(out=junk[:, 0:n], in_=src_ap, func=AF.Square, accum_out=ss[:, col:col + 1]),
            reads=src_keys, writes=['junk', 'ss%d' % col])
        act(lambda: A.activation(out=sd[:, col:col + 1], in_=ss[:, col:col + 1], func=AF.Sqrt,
                                 scale=1.0 / n, bias=eps_t[:, 0:1]),
            reads=['ss%d' % col, 'eps'], writes=['sd%d' % col])
        dve(lambda: V.reciprocal(out=rs[:, col:col + 1], in_=sd[:, col:col + 1]),
            reads=['sd%d' % col], writes=['rs%d' % col])

    set_arena(A_WORK)
    hbs = [sb("hb%d" % i, [128, DM], BF16) for i in range(2)]
    hTts = [sb("hTt%d" % i, [128, 8, 128], BF16) for i in range(2)]
    sq = sb("sq", [128, DM])
    qkn = sb("qkn", [128, DM])
    qkb = sb("qkb", [128, DM], BF16)
    vbs = [sb("vb%d" % i, [128, 8, 65], BF16) for i in range(2)]
    ss8 = sb("ss8", [128, 8])
    sd8 = sb("sd8", [128, 8])
    r8 = sb("r8", [128, 8])
    svn = sb("svn", [128, 8, 64], BF16)
    t1 = sb("t1", [128, 512])
    gated = sb("gated", [128, 512])
    gnb = sb("gnb", [128, 512], BF16)
    for i in range(2):
        dve(lambda i=i: V.memset(vbs[i][:], 1.0), writes=['vb%d' % i])

    def ab_F1(i):
        p = i % 2
        xt, kx, hb, khb, hTt, kh = xts[p], 'xt%d' % p, hbs[p], 'hb%d' % p, hTts[p], 'hTt%d' % p
        rms_rstd(xt[:], [kx], DM, 0)
        dve(lambda: V.scalar_tensor_tensor(out=hb[:], in0=xt[:], scalar=rs[:, 0:1], in1=gA[:],
                                           op0=ALU.mult, op1=ALU.mult),
            reads=[kx, 'rs0', 'gA'], writes=[khb])
        pT = bank_bf(0)
        for c in range(8):
            pe(lambda c=c: T.transpose(out=pT[:, c, :], in_=hb[:, c * 128:(c + 1) * 128], identity=ident_b[:]),
               reads=[khb, 'ident_b'], writes=['b0'])
        act(lambda: A.copy(out=hTt[:], in_=pT), reads=['b0'], writes=[kh])

    def ab_P(i):
        p = i % 2
        hTt, kh = hTts[p], 'hTt%d' % p
        dsts = [1, 2, 3, 4 + 2 * p, 5 + 2 * p]
        for j in range(5):
            for c in range(8):
                pe(lambda j=j, c=c: T.matmul(bank(dsts[j]), lhsT=hTt[:, c, :], rhs=W1[:, c, j * 512:(j + 1) * 512],
                                             start=(c == 0), stop=(c == 7)),
                   reads=[kh, 'RC', 'RCb'], writes=['b%d' % dsts[j]])

    def ab_B1(i):
        p = i % 2
        tsl = slice(i * 128, (i + 1) * 128)
        pqk = bank(1, 2)
        psv = bank(5 + 2 * p)
        ksv = 'b%d' % (5 + 2 * p)
        act(lambda: A.activation(out=sq[:], in_=pqk, func=AF.Square), reads=['b1', 'b2'], writes=['sq'])
        dve(lambda: V.tensor_reduce(out=ss16[:], in_=sq[:].rearrange("p (a b) -> p a b", b=64), axis=AX.X, op=ALU.add),
            reads=['sq'], writes=['ss16'])
        act(lambda: A.activation(out=sd16[:], in_=ss16[:], func=AF.Sqrt, scale=1.0 / 64, bias=eps_t[:, 0:1]),
            reads=['ss16', 'eps'], writes=['sd16'])
        dve(lambda: V.reciprocal(out=r16[:], in_=sd16[:]), reads=['sd16'], writes=['r16'])
        dve(lambda: V.tensor_tensor(out=qkn[:].rearrange("p (a b) -> p a b", b=64),
                                    in0=pqk.rearrange("p (a b) -> p a b", b=64),
                                    in1=r16[:].unsqueeze(2).to_broadcast([128, 16, 64]), op=ALU.mult),
            reads=['b1', 'b2', 'r16'], writes=['qkn'])
        vb = vbs[p]
        kv = 'vb%d' % p
        act(lambda: A.copy(out=vb[:, :, 0:64], in_=bank(3).rearrange("p (h e) -> p h e", e=64)),
            reads=['b3'], writes=[kv])
        dma('pool', Vs[tsl, :], vb[:].rearrange("p h e -> p (h e)"), reads=[kv], writes=['Vs'])
        act(lambda: A.activation(out=sq[:, 0:512], in_=psv, func=AF.Square), reads=[ksv], writes=['sq'])
        dve(lambda: V.tensor_reduce(out=ss8[:], in_=sq[:, 0:512].rearrange("p (a b) -> p a b", b=64), axis=AX.X,
                                    op=ALU.add), reads=['sq'], writes=['ss8'])
        act(lambda: A.activation(out=sd8[:], in_=ss8[:], func=AF.Sqrt, scale=1.0 / 64, bias=eps_t[:, 0:1]),
            reads=['ss8', 'eps'], writes=['sd8'])
        dve(lambda: V.reciprocal(out=r8[:], in_=sd8[:]), reads=['sd8'], writes=['r8'])
        dve(lambda: V.tensor_tensor(out=svn[:], in0=psv.rearrange("p (a b) -> p a b", b=64),
                                    in1=r8[:].unsqueeze(2).to_broadcast([128, 8, 64]), op=ALU.mult),
            reads=[ksv, 'r8'], writes=['svn'])

    def ab_B2(i):
        p = i % 2
        tsl = slice(i * 128, (i + 1) * 128)
        psu, pmx = bank(4 + 2 * p), bank(5 + 2 * p)
        ksu, kmx = 'b%d' % (4 + 2 * p), 'b%d' % (5 + 2 * p)
        dve(lambda: V.tensor_tensor(out=qkb[:], in0=qkn[:], in1=gqk[:], op=ALU.mult),
            reads=['qkn', 'gqk'], writes=['qkb'])
        pT2 = bank_bf(0)
        for c in range(8):
            pe(lambda c=c: T.transpose(out=pT2[:, c, :], in_=qkb[:, c * 128:(c + 1) * 128], identity=ident_b[:]),
               reads=['qkb', 'ident_b'], writes=['b0'])
        act(lambda: A.copy(out=qkT[:, :, tsl], in_=pT2), reads=['b0'], writes=['qkT'])
        for g in range(8):
            pe(lambda g=g: T.matmul(pmx[:, g * 64:(g + 1) * 64], lhsT=WsT[:, g, :], rhs=svn[:, g, :],
                                    start=True, stop=True), reads=['WsT', 'svn'], writes=[kmx])
        dve(lambda: V.tensor_tensor(out=t1[:], in0=pmx, in1=gsv[:], op=ALU.mult),
            reads=[kmx, 'gsv'], writes=['t1'])
        dve(lambda: V.tensor_tensor(out=t1[:].rearrange("p (a b) -> p a b", b=64),
                                    in0=t1[:].rearrange("p (a b) -> p a b", b=64),
                                    in1=bcol[:].unsqueeze(2).to_broadcast([128, 8, 64]), op=ALU.add),
            reads=['t1', 'bcol'], writes=['t1'])
        dve(lambda: V.tensor_tensor(out=gated[:], in0=t1[:], in1=psu, op=ALU.mult),
            reads=['t1', ksu], writes=['gated'])
        rms_rstd(gated[:], ['gated'], 512, 1)
        dve(lambda: V.scalar_tensor_tensor(out=gnb[:], in0=gated[:], scalar=rs[:, 1:2], in1=gso[:],
                                           op0=ALU.mult, op1=ALU.mult),
            reads=['gated', 'rs1', 'gso'], writes=['gnb'])
        dma('pool', Gs[tsl, :], gnb[:], reads=['gnb'], writes=['Gs'])

    def ab_load(i):
        dma('sp', xts[i % 2][:], x_d[i * 128:(i + 1) * 128, :], writes=['xt%d' % (i % 2)])

    ab_load(0)
    ab_load(1)
    ab_F1(0)
    ab_P(0)
    for i in range(NT):
        if i + 1 < NT:
            ab_F1(i + 1)
        if i + 2 < NT:
            ab_load(i + 2)
        ab_B1(i)
        ab_B2(i)
        cast_piece(i)
        if i + 1 < NT:
            ab_P(i + 1)

    dma('sp', bm, bm_d, writes=['RC', 'RCb'])
    Sbs = [sb("Sb%d" % i, [128, 2, 1024]) for i in range(2)]
    PTs = [sb("PTt%d" % i, [128, 2, 1024], BF16) for i in range(2)]
    vts = [sb("vt%d" % i, [128, 8, 65], BF16) for i in range(3)]
    Osb = [sb("Osb%d" % i, [128, 520]) for i in range(2)]
    iters = []
    for b, d in enumerate([1, 4, 16]):
        for r in range(d):
            for n in range(32 // d):
                iters.append((b, d, r, n))
    units = []
    for it, (b, d, r, n) in enumerate(iters):
        for w in ([0, 1] if n > 0 else [1]):
            units.append((it, w))

    def d_tok(it):
        b, d, r, n = iters[it]
        q0 = n * 128 * d + r
        return slice(q0, q0 + 127 * d + 1, d)

    def d_V(it):
        cur = it % 3
        dma('sp', vts[cur][:].rearrange("p h e -> p (h e)"), Vs[d_tok(it), :], reads=['Vs'], writes=['vt%d' % cur])

    def d_S(ui):
        it, w = units[ui]
        b, d, r, n = iters[it]
        qtok = d_tok(it)
        k0 = (n - 1 + w) * 128 * d + r
        ktok = slice(k0, k0 + 127 * d + 1, d)
        bb = 2 * (ui % 2)
        for par in range(2):
            for hp in range(4):
                o = bank(bb + par)[:, hp * 128:(hp + 1) * 128]
                pe(lambda o=o, par=par, hp=hp: T.matmul(
                    o, lhsT=qkT[par * 64:(par + 1) * 64, 4 + hp, ktok],
                    rhs=qkT[par * 64:(par + 1) * 64, hp, qtok], start=True, stop=True),
                   reads=['qkT'], writes=['b%d' % (bb + par)])

    def d_BE(ui):
        it, w = units[ui]
        b = iters[it][0]
        bb = 2 * (ui % 2)
        Sb, PTt = Sbs[it % 2], PTs[it % 2]
        ks, kp = 'Sb%d_%d' % (it % 2, w), 'PT%d_%d' % (it % 2, w)
        dve(lambda: V.scalar_tensor_tensor(
            out=Sb[:, w, :], in0=bank(bb, 2), scalar=0.125,
            in1=bm[:, (b * 2 + w) * 1024:(b * 2 + w + 1) * 1024], op0=ALU.mult, op1=ALU.add),
            reads=['b%d' % bb, 'b%d' % (bb + 1), 'RC'], writes=[ks])
        act(lambda: A.activation(out=PTt[:, w, :], in_=Sb[:, w, :], func=AF.Exp), reads=[ks], writes=[kp])

    def d_PV(it):
        b, d, r, n = iters[it]
        ws = [0, 1] if n > 0 else [1]
        cur, prv = it % 3, (it - 1) % 3
        PTt = PTs[it % 2]
        ob = 4 + 2 * (it % 2)
        for h in range(8):
            par, hp = h % 2, h // 2
            o = bank(ob + h // 4)[:, (h % 4) * 65:(h % 4) * 65 + 65]
            for wi, w in enumerate(ws):
                vt = vts[cur] if w == 1 else vts[prv]
                kvt = 'vt%d' % (cur if w == 1 else prv)
                col = (par * 4 + hp) * 128
                pe(lambda o=o, w=w, col=col, vt=vt, h=h, wi=wi: T.matmul(
                    o, lhsT=PTt[:, w, col:col + 128], rhs=vt[:, h, :],
                    start=(wi == 0), stop=(wi == len(ws) - 1)),
                   reads=['PT%d_%d' % (it % 2, w), kvt], writes=['b%d' % (ob + h // 4)])
        osb = Osb[it % 2]
        ko = 'Osb%d' % (it % 2)
        act(lambda: A.copy(out=osb[:, 0:260], in_=bank(ob)[:, 0:260]), reads=['b%d' % ob], writes=[ko])
        act(lambda: A.copy(out=osb[:, 260:520], in_=bank(ob + 1)[:, 0:260]), reads=['b%d' % (ob + 1)], writes=[ko])
        dma('pool', OB[b, d_tok(it), :], osb[:], reads=[ko], writes=['OB'])

    d_V(0)
    d_S(0)
    for ui, (it, w) in enumerate(units):
        if ui + 1 < len(units):
            nit, nw = units[ui + 1]
            if nit != it:
                d_V(nit)
            d_S(ui + 1)
        d_BE(ui)
        if ui + 1 == len(units) or units[ui + 1][0] != it:
            d_PV(it)

    S.barrier()
    dma('pool', Wo, w_out_d.rearrange("(c p) n -> p c n", p=128), writes=['RC'])
    dma('sp', gA[:], g2_d.partition_broadcast(128), writes=['gA'])
    obts, accs, rdens, attns, mixbs, mixTs = [], [], [], [], [], []
    for p in range(2):
        set_arena(A_WORK if p == 0 else (78848 + 16384, 119808))
        if p == 0:
            set_arena(A_WORK)
        obts.append(sb("obt%d" % p, [128, 3, 520]))
        accs.append(sb("acc%d" % p, [128, 520]))
        rdens.append(sb("rden%d" % p, [128, 8, 1]))
        attns.append(sb("attn%d" % p, [128, 512]))
        mixbs.append(sb("mixb%d" % p, [128, DM], BF16))
        mixTs.append(sb("mixT%d" % p, [128, 8, 128], BF16))
        if p == 0:
            x2 = [sb("x2_%d" % i, [128, DM]) for i in range(2)]
            h2b = sb("h2b", [128, DM], BF16)
            if stage >= 2:
                WqE = sb("WqE", [128, 8, 2048], BF16)
                h2t_sb = sb("h2t_sb", [128, 8, 128], BF16)
                qT_sb = sb("qT_sb", [128, 2048], BF16)
    if stage >= 2:
        dma('pool', WqE, w_q_d.rearrange("(c p) n -> p c n", p=128), writes=['WqE'])

    def e_E1(i):
        p = i % 2
        tsl = slice(i * 128, (i + 1) * 128)
        obt, acc, rden, attn, mixb, mixT = obts[p], accs[p], rdens[p], attns[p], mixbs[p], mixTs[p]
        ko, ka, kr, kt, kma, kms, kmt = ('obt%d' % p, 'acc%d' % p, 'rden%d' % p, 'attn%d' % p, 'mixb_a%d' % p,
                                         'mixb_s%d' % p, 'mixT%d' % p)
        acc3 = acc[:].rearrange("p (h e) -> p h e", e=65)
        dma('sp', obt[:], OB[:, tsl, :].rearrange("b t c -> t b c"), reads=['OB'], writes=[ko])
        dma('sp', mixb[:, 512:1024], Gs[tsl, :], reads=['Gs'], writes=[kms])
        dve(lambda: V.tensor_tensor(out=acc[:], in0=obt[:, 0, :], in1=obt[:, 1, :], op=ALU.add),
            reads=[ko], writes=[ka])
        dve(lambda: V.tensor_tensor(out=acc[:], in0=acc[:], in1=obt[:, 2, :], op=ALU.add),
            reads=[ko, ka], writes=[ka])
        dve(lambda: V.reciprocal(out=rden[:], in_=acc3[:, :, 64:65]), reads=[ka], writes=[kr])
        dve(lambda: V.tensor_tensor(out=attn[:].rearrange("p (h e) -> p h e", e=64), in0=acc3[:, :, 0:64],
                                    in1=rden[:].to_broadcast([128, 8, 64]), op=ALU.mult),
            reads=[ka, kr], writes=[kt])
        rms_rstd(attn[:], [kt], 512, 2)
        dve(lambda: V.scalar_tensor_tensor(out=mixb[:, 0:512], in0=attn[:], scalar=rs[:, 2:3], in1=gao[:],
                                           op0=ALU.mult, op1=ALU.mult),
            reads=[kt, 'rs2', 'gao'], writes=[kma])
        pT = bank_bf(0)
        for c in range(8):
            pe(lambda c=c: T.transpose(out=pT[:, c, :], in_=mixb[:, c * 128:(c + 1) * 128], identity=ident_b[:]),
               reads=[kma if c < 4 else kms, 'ident_b'], writes=['b0'])
        act(lambda: A.copy(out=mixT[:], in_=pT), reads=['b0'], writes=[kmt])

    def e_E2(i):
        p = i % 2
        tsl = slice(i * 128, (i + 1) * 128)
        mixT, kmt = mixTs[p], 'mixT%d' % p
        xt, kx = xts[p], 'xt%d' % p
        xx, k2 = x2[p], 'x2_%d' % p
        dma('sp', xt[:], x_d[tsl, :], writes=[kx])
        for half in range(2):
            for c in range(8):
                pe(lambda half=half, c=c: T.matmul(bank(1 + half), lhsT=mixT[:, c, :],
                                                   rhs=Wo[:, c, half * 512:(half + 1) * 512],
                                                   start=(c == 0), stop=(c == 7)),
                   reads=[kmt, 'RC'], writes=['b%d' % (1 + half)])
        dve(lambda: V.tensor_tensor(out=xx[:], in0=bank(1, 2), in1=xt[:], op=ALU.add),
            reads=['b1', 'b2', kx], writes=[k2])
        dma('pool', out_d[tsl, :], xx[:], reads=[k2], writes=['out%d' % i])
        if stage >= 2:
            rms_rstd(xx[:], [k2], DM, 3)
            dve(lambda: V.scalar_tensor_tensor(out=h2b[:], in0=xx[:], scalar=rs[:, 3:4], in1=gA[:],
                                               op0=ALU.mult, op1=ALU.mult),
                reads=[k2, 'rs3', 'gA'], writes=['h2b'])
            pT3 = bank_bf(3)
            for c in range(8):
                pe(lambda c=c: T.transpose(out=pT3[:, c, :], in_=h2b[:, c * 128:(c + 1) * 128], identity=ident_b[:]),
                   reads=['h2b', 'ident_b'], writes=['b3'])
            act(lambda: A.copy(out=h2t_sb[:], in_=pT3), reads=['b3'], writes=['h2t_sb'])
            dma('pool', H2S[i], h2t_sb[:].rearrange("p a b -> p (a b)"), reads=['h2t_sb'], writes=['H2S%d' % i])

    def e_E3(i):
        if stage < 2:
            return
        for j in range(16):
            for c in range(8):
                pe(lambda j=j, c=c: T.matmul(bank(4 + j // 4)[:, (j % 4) * 128:(j % 4 + 1) * 128],
                                             lhsT=WqE[:, c, j * 128:(j + 1) * 128], rhs=h2t_sb[:, c, :],
                                             start=(c == 0), stop=(c == 7)),
                   reads=['WqE', 'h2t_sb'], writes=['b%d' % (4 + j // 4)])
        act(lambda: A.copy(out=qT_sb[:], in_=bank(4, 4)), reads=['b4', 'b5', 'b6', 'b7'], writes=['qT_sb'])
        dma('pool', QS[i], qT_sb[:], reads=['qT_sb'], writes=['QS%d' % i])

    e_E1(0)
    for i in range(NT):
        e_E2(i)
        if i + 1 < NT:
            e_E1(i + 1)
        e_E3(i)
        cast_piece(32 + i)

    if stage >= 2:
        S.barrier()
        peer(nc, S, locals())

    S.final_wait('sp')
    S.emit()
    return nc


def peer(nc, S, L):
    V, A, T, G, SPQ = nc.vector, nc.scalar, nc.tensor, nc.gpsimd, nc.sync
    sb, bank, dma, dve, act, pe, pool = L['sb'], L['bank'], L['dma'], L['dve'], L['act'], L['pe'], L['pool']
    keysT = L['keysT']
    ident_f, iota_f, c16 = L['ident_f'], L['iota_f'], L['c16']
    out_d, UTb, EVb, stage, H2S, QS = L['out_d'], L['UTb'], L['EVb'], L['stage'], L['H2S'], L['QS']
    xts = L['xts']
    set_arena = L['set_arena']

    set_arena((13312, 78848))
    GT = sb("GT", [128, TG, 128], BF16)
    set_arena((78848, 208000))
    e1T = sb("e1T", [128, SEQ], BF16)
    e2T = sb("e2T", [128, SEQ], BF16)
    gTt = sb("gTt", [128, SEQ], F32)
    qTs = sb("qTs", [128, 16, 128], BF16)
    ssb = sb("ssb", [128, 16, 128])
    tmp = [sb("tmp%d" % i, [128, 256]) for i in range(2)]
    v16 = sb("v16", [128, 16, 16])
    idx = sb("idx", [128, 16, 16], U32)
    idxf = sb("idxf", [128, 16, 16])
    gq = sb("gq", [128, 4096], BF16)
    cand = gq.bitcast(F32).rearrange("p (h c) -> p h c", h=8)
    CK = ['cand', 'ge4', 'eq4']
    best = sb("best", [128, 8, 16])
    pos = sb("pos", [128, 8, 16], U32)
    posf = sb("posf", [128, 8, 16])
    eb = sb("eb", [128, 8, 16])
    esum = sb("esum", [128, 8])
    rsum = sb("rsum", [128, 8])
    gts = [sb("gt%d" % i, [128, 128]) for i in range(2)]
    ge4 = gq[:, 0:2048].rearrange("p (a b c) -> p a b c", a=8, b=16)
    eq4 = gq[:, 2048:4096].rearrange("p (a b c) -> p a b c", a=8, b=16)
    r1raw = sb("r1raw", [128, 8, 16])
    r2t = sb("r2t", [128, 8, 16])
    e1s = [sb("e1_%d" % i, [128, 128]) for i in range(2)]
    e2s = [sb("e2_%d" % i, [128, 128]) for i in range(2)]
    thr16 = c16[:, 0:16]
    io16p1 = c16[:, 16:32]
    io16m16 = c16[:, 32:48]
    v4 = v16[:].rearrange("p (h two) r -> p h two r", two=2)
    i4 = idxf[:].rearrange("p (h two) r -> p h two r", two=2)
    cand4 = cand[:].rearrange("p h (a b) -> p h a b", b=16)

    def F_tile(i):
        tsl = slice(i * 128, (i + 1) * 128)
        e1, e2, gt = e1s[i % 2], e2s[i % 2], gts[i % 2]
        ke1, ke2, kgt = 'e1_%d' % (i % 2), 'e2_%d' % (i % 2), 'gt%d' % (i % 2)
        dma('sp', qTs[:].rearrange("p a b -> p (a b)"), QS[i], reads=['QS%d' % i], writes=['qTs'])
        yield None
        for qd in range(4):
            for jj in range(4):
                j = qd * 4 + jj
                pe(lambda j=j, jj=jj: T.matmul(bank(7)[:, jj * 128:(jj + 1) * 128], lhsT=qTs[:, j, :],
                                               rhs=keysT[:, j % 2, :], start=True, stop=True),
                   reads=['qTs', 'keysT'], writes=['b7'])
            act(lambda qd=qd: A.copy(out=ssb[:, qd * 4:(qd + 1) * 4, :].rearrange("p a b -> p (a b)"), in_=bank(7)),
                reads=['b7'], writes=['ssb%d' % qd])
            yield None
        for j in range(16):
            tm = tmp[j % 2]
            kt = 'tmp%d' % (j % 2)
            ks = 'ssb%d' % (j // 4)
            dve(lambda j=j: V.max(out=v16[:, j, 0:8], in_=ssb[:, j, :]), reads=[ks], writes=['v16'])
            dve(lambda j=j, tm=tm: V.match_replace(out=tm[:, 0:128], in_to_replace=v16[:, j, 0:8],
                                                   in_values=ssb[:, j, :], imm_value=-1e30),
                reads=[ks, 'v16'], writes=[kt])
            dve(lambda j=j, tm=tm: V.max(out=v16[:, j, 8:16], in_=tm[:, 0:128]), reads=[kt], writes=['v16'])
            dve(lambda j=j: V.max_index(out=idx[:, j, 0:8], in_max=v16[:, j, 0:8], in_values=ssb[:, j, :]),
                reads=[ks, 'v16'], writes=['idx'])
            dve(lambda j=j: V.max_index(out=idx[:, j, 8:16], in_max=v16[:, j, 8:16], in_values=ssb[:, j, :]),
                reads=[ks, 'v16'], writes=['idx'])
            if j % 2 == 1:
                yield None
        dve(lambda: V.tensor_copy(out=idxf[:], in_=idx[:]), reads=['idx'], writes=['idxf'])
        dve(lambda: V.tensor_tensor(out=cand4, in0=v4[:, :, 0, :].unsqueeze(3).to_broadcast([128, 8, 16, 16]),
                                    in1=v4[:, :, 1, :].unsqueeze(2).to_broadcast([128, 8, 16, 16]), op=ALU.add),
            reads=['v16'], writes=CK)
        yield None
        for h in range(8):
            tm = tmp[h % 2]
            kt = 'tmp%d' % (h % 2)
            dve(lambda h=h: V.max(out=best[:, h, 0:8], in_=cand[:, h, :]), reads=CK, writes=['best'])
            dve(lambda h=h, tm=tm: V.match_replace(out=tm[:], in_to_replace=best[:, h, 0:8], in_values=cand[:, h, :],
                                                   imm_value=-1e30), reads=CK + ['best'], writes=[kt])
            dve(lambda h=h, tm=tm: V.max(out=best[:, h, 8:16], in_=tm[:]), reads=[kt], writes=['best'])
            dve(lambda h=h: V.max_index(out=pos[:, h, 0:8], in_max=best[:, h, 0:8], in_values=cand[:, h, :]),
                reads=CK + ['best'], writes=['pos'])
            dve(lambda h=h: V.max_index(out=pos[:, h, 8:16], in_max=best[:, h, 8:16], in_values=cand[:, h, :]),
                reads=CK + ['best'], writes=['pos'])
            if h % 2 == 1:
                yield None
        dve(lambda: V.tensor_tensor(out=eb[:], in0=best[:], in1=best[:, :, 0:1].to_broadcast([128, 8, 16]),
                                    op=ALU.subtract), reads=['best'], writes=['eb'])
        act(lambda: A.activation(out=eb[:], in_=eb[:], func=AF.Exp), reads=['eb'], writes=['eb'])
        dve(lambda: V.tensor_reduce(out=esum[:], in_=eb[:], axis=AX.X, op=ALU.add), reads=['eb'], writes=['esum'])
        dve(lambda: V.reciprocal(out=rsum[:], in_=esum[:]), reads=['esum'], writes=['rsum'])
        dve(lambda: V.tensor_tensor(out=gt[:].rearrange("p (h k) -> p h k", k=16), in0=eb[:],
                                    in1=rsum[:].unsqueeze(2).to_broadcast([128, 8, 16]), op=ALU.mult),
            reads=['eb', 'rsum'], writes=[kgt])
        yield None
        dve(lambda: V.tensor_copy(out=posf[:], in_=pos[:]), reads=['pos'], writes=['posf'])
        dve(lambda: V.tensor_tensor(out=ge4[:], in0=posf[:].unsqueeze(3).to_broadcast([128, 8, 16, 16]),
                                    in1=thr16[:, None, None, :].to_broadcast([128, 8, 16, 16]), op=ALU.is_ge),
            reads=['posf', 'c16'], writes=['ge4'])
        dve(lambda: V.tensor_reduce(out=r1raw[:], in_=ge4[:], axis=AX.X, op=ALU.add), reads=['ge4'], writes=['r1raw'])
        dve(lambda: V.tensor_tensor(out=eq4[:], in0=r1raw[:].unsqueeze(3).to_broadcast([128, 8, 16, 16]),
                                    in1=io16p1[:, None, None, :].to_broadcast([128, 8, 16, 16]), op=ALU.is_equal),
            reads=['r1raw', 'c16'], writes=['eq4'])
        dve(lambda: V.tensor_tensor(out=eq4[:], in0=eq4[:],
                                    in1=i4[:, :, 0, :].unsqueeze(2).to_broadcast([128, 8, 16, 16]), op=ALU.mult),
            reads=['eq4', 'idxf'], writes=['eq4'])
        dve(lambda: V.tensor_reduce(out=e1[:].rearrange("p (h k) -> p h k", k=16), in_=eq4[:], axis=AX.X, op=ALU.add),
            reads=['eq4'], writes=[ke1])
        dve(lambda: V.scalar_tensor_tensor(out=r2t[:], in0=r1raw[:], scalar=-16.0, in1=posf[:],
                                           op0=ALU.mult, op1=ALU.add), reads=['r1raw', 'posf'], writes=['r2t'])
        dve(lambda: V.tensor_tensor(out=ge4[:], in0=r2t[:].unsqueeze(3).to_broadcast([128, 8, 16, 16]),
                                    in1=io16m16[:, None, None, :].to_broadcast([128, 8, 16, 16]), op=ALU.is_equal),
            reads=['r2t', 'c16'], writes=['ge4'])
        dve(lambda: V.tensor_tensor(out=ge4[:], in0=ge4[:],
                                    in1=i4[:, :, 1, :].unsqueeze(2).to_broadcast([128, 8, 16, 16]), op=ALU.mult),
            reads=['ge4', 'idxf'], writes=['ge4'])
        dve(lambda: V.tensor_reduce(out=e2[:].rearrange("p (h k) -> p h k", k=16), in_=ge4[:], axis=AX.X, op=ALU.add),
            reads=['ge4'], writes=[ke2])
        yield 'tail'
        for m, (src, ks) in enumerate([(e1, ke1), (e2, ke2), (gt, kgt)]):
            pe(lambda m=m, src=src: T.transpose(out=bank(7)[:, m * 128:(m + 1) * 128], in_=src[:], identity=ident_f[:]),
               reads=[ks, 'ident_f'], writes=['b7'])
        act(lambda tsl=tsl: A.copy(out=e1T[:, tsl], in_=bank(7)[:, 0:128]), reads=['b7'], writes=['e1T%d' % i])
        act(lambda tsl=tsl: A.copy(out=e2T[:, tsl], in_=bank(7)[:, 128:256]), reads=['b7'], writes=['e2T%d' % i])
        act(lambda tsl=tsl: A.copy(out=gTt[:, tsl], in_=bank(7)[:, 256:384]), reads=['b7'], writes=['gT%d' % i])

    def exhaust(gen):
        for _ in gen:
            pass

    if stage < 3:
        for i in range(NT):
            exhaust(F_tile(i))
        return

    SBK = 16
    Xs = [sb("X%d" % i, [128, SBK, 128], BF16) for i in range(2)]
    Yts = [sb("Yt%d" % i, [128, SBK, 128], BF16) for i in range(2)]
    Ys = [sb("Y%d" % i, [128, SBK, 128], BF16) for i in range(2)]
    NB = 6
    uts = [sb("ut%d" % i, [128, 8, 128], BF16) for i in range(NB)]
    vcs = [sb("vc%d" % i, [128, DM], BF16) for i in range(NB)]
    ges = [sb("ge%d" % i, [128, TG], BF16) for i in range(3)]
    cts = [sb("ct%d" % i, [128, TG], BF16) for i in range(3)]
    res = Yts[1].rearrange("p a b -> p (a b)").bitcast(F32)
    h2g = [sb("h2g%d" % i, [128, 8, TG], BF16) for i in range(2)]
    NG = SEQ // TG
    ROWS = 16384 // 64
    blk = 0
    gstate = {'gcnt': 0}

    def load_h2g(g):
        for tt in range(2):
            dma('sp', h2g[g % 2][:, :, tt * 128:(tt + 1) * 128],
                H2S[2 * g + tt].rearrange("p (a b) -> p a b", a=8),
                reads=['H2S%d' % (2 * g + tt)], writes=['h2g%d' % (g % 2)])

    exhaust(F_tile(0))
    exhaust(F_tile(1))
    load_h2g(0)
    for g in range(NG):
        t0 = g * TG
        nblk_g = TG // SBK

        def gb_s1(s_):
            bi = blk + s_
            ts0 = t0 + s_ * SBK
            ti = ts0 // 128
            X, Yt = Xs[bi % 2], Yts[bi % 2]
            dve(lambda: V.tensor_tensor(
                out=X[:], in0=iota_f[:, None, :].to_broadcast([128, SBK, 128]),
                in1=e1T[:, ts0:ts0 + SBK].unsqueeze(2).to_broadcast([128, SBK, 128]), op=ALU.is_equal),
                reads=['iota_f', 'e1T%d' % ti], writes=['X%d' % (bi % 2)])
            dve(lambda: V.tensor_tensor(
                out=Yt[:], in0=iota_f[:, None, :].to_broadcast([128, SBK, 128]),
                in1=e2T[:, ts0:ts0 + SBK].unsqueeze(2).to_broadcast([128, SBK, 128]), op=ALU.is_equal),
                reads=['iota_f', 'e2T%d' % ti], writes=['Yt%d' % (bi % 2)])

        def gb_s23(s_):
            nonlocal_gcnt = gstate['gcnt']
            bi = blk + s_
            ts0 = t0 + s_ * SBK
            ti = ts0 // 128
            X, Yt, Y = Xs[bi % 2], Yts[bi % 2], Ys[bi % 2]
            kX, kYt = 'X%d' % (bi % 2), 'Yt%d' % (bi % 2)
            for tt in range(10):
                act(lambda tt=tt: A.activation(out=Y[:, tt, :], in_=Yt[:, tt, :], func=AF.Copy,
                                               scale=gTt[:, ts0 + tt:ts0 + tt + 1]),
                    reads=[kYt, 'gT%d' % ti], writes=['Y%d_%d' % (bi % 2, tt)])
            dve(lambda: V.tensor_tensor(out=Y[:, 10:16, :], in0=Yt[:, 10:16, :],
                                        in1=gTt[:, ts0 + 10:ts0 + 16].unsqueeze(2).to_broadcast([128, 6, 128]),
                                        op=ALU.mult),
                reads=[kYt, 'gT%d' % ti], writes=['Y%d_%d' % (bi % 2, tt) for tt in range(10, 16)])
            for oct_ in range(SBK // 8):
                gc = gstate['gcnt']
                pb = 4 + 2 * (gc % 2)
                for u in range(8):
                    tt = oct_ * 8 + u
                    o = bank(pb + u // 4)[:, (u % 4) * 128:(u % 4 + 1) * 128]
                    pe(lambda o=o, tt=tt: T.matmul(o, lhsT=Y[:, tt, :], rhs=X[:, tt, :], start=True, stop=True),
                       reads=[kX, 'Y%d_%d' % (bi % 2, tt)], writes=['b%d' % (pb + u // 4)])
                tl = s_ * SBK + oct_ * 8
                if True:
                    act(lambda pb=pb, tl=tl: A.copy(out=GT[:, tl:tl + 8, :].rearrange("p t i -> p (t i)"),
                                                    in_=bank(pb, 2)),
                        reads=['b%d' % pb, 'b%d' % (pb + 1)], writes=['GT'])
                else:
                    dve(lambda pb=pb, tl=tl: V.tensor_copy(out=GT[:, tl:tl + 8, :].rearrange("p t i -> p (t i)"),
                                                          in_=bank(pb, 2)),
                        reads=['b%d' % pb, 'b%d' % (pb + 1)], writes=['GT'])
                gstate['gcnt'] += 1

        if not gstate.get('pre_s1'):
            gb_s1(0)
        gstate['pre_s1'] = False
        for s_ in range(nblk_g):
            if s_ + 1 < nblk_g:
                gb_s1(s_ + 1)
            gb_s23(s_)
        blk += nblk_g

        hg = h2g[g % 2]
        khg = 'h2g%d' % (g % 2)

        def load_chunk(c):
            ut, vc = uts[c % NB], vcs[c % NB]
            dma('sp', ut[:].rearrange("p a b -> p (a b)"), UTb[c * 128:(c + 1) * 128, :],
                reads=['UTb%d' % (c * 128 // ROWS)], writes=['ut%d' % (c % NB)])
            dma('sp', vc[:], EVb[c * 128:(c + 1) * 128, :],
                reads=['EVb%d' % (c * 128 // ROWS)], writes=['vc%d' % (c % NB)])

        def a_mm(c, hg=hg, khg=khg):
            ut = uts[c % NB]
            pa = bank(4 + c % 3)[:, 0:TG]
            for dc in range(8):
                pe(lambda ut=ut, pa=pa, dc=dc, hg=hg: T.matmul(pa, lhsT=ut[:, dc, :], rhs=hg[:, dc, :],
                                                               start=(dc == 0), stop=(dc == 7)),
                   reads=['ut%d' % (c % NB), khg], writes=['b%d' % (4 + c % 3)])

        gens = [F_tile(2 * g + 2), F_tile(2 * g + 3)] if g + 1 < NG else []
        tails = []
        state = {'cur': 0}

        def step():
            while state['cur'] < len(gens):
                try:
                    r = next(gens[state['cur']])
                except StopIteration:
                    state['cur'] += 1
                    continue
                if r == 'tail':
                    tails.append(gens[state['cur']])
                    state['cur'] += 1
                    continue
                return

        for c0 in range(5):
            load_chunk(c0)
        def gelu_mult(c):
            gel, ct = ges[c % 3], cts[c % 3]
            act(lambda: A.activation(out=gel[:], in_=bank(4 + c % 3)[:, 0:TG], func=AF.Gelu),
                reads=['b%d' % (4 + c % 3)], writes=['ge%d' % (c % 3)])
            pool(lambda: G.tensor_tensor(out=ct[:], in0=gel[:], in1=GT[:, :, c], op=ALU.mult),
                 reads=['ge%d' % (c % 3), 'GT'], writes=['ct%d' % (c % 3)])

        a_mm(0)
        a_mm(1)
        gelu_mult(0)
        for c in range(128):
            if c + 5 < 128:
                load_chunk(c + 5)
            if c + 2 < 128:
                a_mm(c + 2)
            if c + 1 < 128:
                gelu_mult(c + 1)
            ct, vc = cts[c % 3], vcs[c % NB]
            for tt in range(2):
                for dh in range(2):
                    pe(lambda ct=ct, vc=vc, tt=tt, dh=dh, c=c: T.matmul(
                        bank(tt * 2 + dh), lhsT=ct[:, tt * 128:(tt + 1) * 128], rhs=vc[:, dh * 512:(dh + 1) * 512],
                        start=(c == 0), stop=(c == 127)),
                       reads=['ct%d' % (c % 3), 'vc%d' % (c % NB)], writes=['b%d' % (tt * 2 + dh)])
            if c >= 4 and c % 2 == 0 and c < 120:
                step()
            if c == 64 and g + 1 < NG:
                load_h2g(g + 1)
            if c == 120:
                while state['cur'] < len(gens):
                    step()
                for tg_ in tails:
                    exhaust(tg_)
        if g + 1 < NG:
            t0 = (g + 1) * TG
            gb_s1(0)
            gstate['pre_s1'] = True
        for tt in range(2):
            i = 2 * g + tt
            tsl = slice(i * 128, (i + 1) * 128)
            xt = xts[tt]
            kx = 'xt%d' % tt
            dma('sp', xt[:], out_d[tsl, :], reads=['out%d' % i], writes=[kx])
            dve(lambda xt=xt, tt=tt: V.tensor_tensor(out=res[:], in0=bank(tt * 2, 2), in1=xt[:], op=ALU.add),
                reads=['b%d' % (tt * 2), 'b%d' % (tt * 2 + 1), kx], writes=['Yt1'])
            dma('sp', out_d[tsl, :], res[:], reads=['Yt1'], writes=['out%d' % i])


_CACHE = {}


def host_consts():
    k = np.arange(128)[:, None]
    q = np.arange(128)[None, :]
    bm = np.zeros((128, 3, 2, 2, 4, 128), np.float32)
    for b, d in enumerate([1, 4, 16]):
        for w in range(2):
            dist = (q - k) if w == 1 else (128 + q - k)
            valid = (dist >= 0) if w == 1 else (dist <= 128)
            for h in range(8):
                slope = 2.0 ** (-(h + 1))
                val = np.where(valid, -slope * d * dist, NEG).astype(np.float32)
                bm[:, b, w, h % 2, h // 2, :] = val
    ident = np.eye(128, dtype=np.float32)
    iota = np.broadcast_to(np.arange(128, dtype=np.float32)[None, :], (128, 128)).copy()
    r = np.arange(16, dtype=np.float32)
    c16 = np.broadcast_to(np.concatenate([16 * r, r + 1, r - 16])[None, :], (128, 48)).copy()
    s = np.arange(128)[:, None]
    t = np.arange(128)[None, :]
    trilT = (t >= s).astype(np.float32)
    return dict(bm=bm.reshape(128, 6144), ident=ident, iota=iota, c16=c16, trilT=trilT)


def make_in_maps(inputs, stage=9):
    f = lambda a: np.ascontiguousarray(np.asarray(a, dtype=np.float32))
    x = f(inputs['x'])
    cs = host_consts()
    eu = f(inputs['expert_u'])[0]
    uT = np.ascontiguousarray(eu.reshape(128, 128, 8, 128).transpose(0, 3, 2, 1)).reshape(128 * 128, DM)
    shared = dict(
        w_in=f(inputs['w_in'])[0], w_out=f(inputs['w_out'])[0], w_query=f(inputs['w_query'])[0],
        g1=f(inputs['norm1_g']).reshape(1, DM), g2=f(inputs['norm2_g']).reshape(1, DM),
        gqk=np.concatenate([np.tile(f(inputs['q_norm_g'])[0], 8), np.tile(f(inputs['k_norm_g'])[0], 8)]).reshape(1, DM),
        gsv=f(inputs['sgu_norm_g']).reshape(1, 512), gao=f(inputs['attn_out_g']).reshape(1, 512),
        gso=f(inputs['sgu_out_g']).reshape(1, 512),
        wsT=np.ascontiguousarray(f(inputs['sgu_w'])[0].transpose(2, 0, 1)),
        bcol=np.ascontiguousarray(f(inputs['sgu_b'])[0].T),
        keysT=np.ascontiguousarray(f(inputs['sub_keys'])[0].transpose(2, 0, 1)),
        uT=uT, ev=f(inputs['expert_v'])[0], **cs)
    return [dict(x=np.ascontiguousarray(x[b]), **shared) for b in range(x.shape[0])]


def kernel(**inputs):
    if 'nc' not in _CACHE:
        _CACHE['nc'] = build_program()
    nc = _CACHE['nc']
    in_maps = make_in_maps(inputs)
    n = len(in_maps)
    res = run_bass_kernel_spmd(nc, in_maps, core_ids=list(range(n)))
    return np.stack([np.asarray(r["out"], dtype=np.float32) for r in res.results], axis=0)
```

```python
import contextlib
import numpy as np
import concourse.bass as bass
import concourse.mybir as mybir
from concourse.bass_utils import run_bass_kernel_spmd

F32 = mybir.dt.float32
BF16 = mybir.dt.bfloat16
U32 = mybir.dt.uint32
AF = mybir.ActivationFunctionType
ALU = mybir.AluOpType
AX = mybir.AxisListType

ENGS = ['pe', 'act', 'dve', 'pool', 'sp']
EPOCH = 30000

SEQ = 4096
DM = 1024
NT = SEQ // 128
EPS = 1e-6
TG = 256
NEG = -30000.0


class Sched:
    NS = 16

    def __init__(self, nc):
        self.nc = nc
        self.q = {e: [] for e in ENGS}
        self.cnt = {e: 0 for e in ENGS}
        self.seen = {e: {} for e in ENGS}
        self.lastw = {}
        self.readers = {}
        self.sems = {}
        self.dma_n = {e: 0 for e in ENGS}
        self.dma_slot_val = {}
        self.dma_slot_tok = {}

    def op(self, eng, fn, reads=(), writes=(), dma=False):
        deps = {}

        def add(tok):
            if tok is None:
                return
            k, v = tok
            if deps.get(k, 0) < v:
                deps[k] = v
        for k in reads:
            add(self.lastw.get(k))
        for k in writes:
            add(self.lastw.get(k))
            for t in self.readers.get(k, ()):
                add(t)
        if dma:
            slot = self.dma_n[eng] % self.NS
            self.dma_n[eng] += 1
            skey = "d_%s_%d" % (eng, slot)
            add(self.dma_slot_tok.get(skey))
            val = self.dma_slot_val.get(skey, 0) + 16
            self.dma_slot_val[skey] = val
            tok = (skey, val)
            self.dma_slot_tok[skey] = tok
            inc = 16
        else:
            c = self.cnt[eng]
            self.cnt[eng] += 1
            skey = "c_%s_%d" % (eng, c // EPOCH)
            tok = (skey, c % EPOCH + 1)
            inc = 1
        waits = []
        for k, v in deps.items():
            if eng == 'pe' and k.startswith('c_pe'):
                continue
            if self.seen[eng].get(k, 0) < v:
                self.seen[eng][k] = v
                waits.append((k, v))
        self.q[eng].append((waits, fn, skey, inc))
        for k in reads:
            self.readers.setdefault(k, []).append(tok)
        for k in writes:
            self.lastw[k] = tok
            self.readers[k] = []
        return tok

    def barrier(self):
        toks = list(self.dma_slot_tok.values())
        for e in ENGS:
            c = self.cnt[e]
            if c > 0:
                toks.append(("c_%s_%d" % (e, (c - 1) // EPOCH), (c - 1) % EPOCH + 1))
        for e in ENGS:
            waits = []
            for k, v in toks:
                if self.seen[e].get(k, 0) < v:
                    self.seen[e][k] = v
                    waits.append((k, v))
            if waits:
                self.q[e].append((waits, None, None, 0))

    def final_wait(self, eng):
        waits = list(self.dma_slot_tok.values())
        self.q[eng].append((waits, None, None, 0))

    def emit(self):
        nc = self.nc
        engmap = {'pe': 'tensor', 'act': 'scalar', 'dve': 'vector', 'pool': 'gpsimd', 'sp': 'sync'}
        stack = contextlib.ExitStack()
        keys = []
        for e in ENGS:
            for (waits, fn, skey, inc) in self.q[e]:
                for k, v in waits:
                    if k not in keys:
                        keys.append(k)
                if skey is not None and skey not in keys:
                    keys.append(skey)
        for k in keys:
            self.sems[k] = stack.enter_context(nc.semaphore("s_" + k))
        with stack, nc.Block() as block:
            for e in ENGS:
                items = self.q[e]

                def body(engine, items=items):
                    for (waits, fn, skey, inc) in items:
                        for k, v in waits:
                            engine.wait_ge(self.sems[k], v)
                        if fn is not None:
                            fn().then_inc(self.sems[skey], inc)
                getattr(block, engmap[e])(body)


def build_program(stage=9, dbg=False):
    nc = bass.Bass("TRN2", target_bir_lowering=False)
    S = Sched(nc)
    V, A, T, G, SPQ = nc.vector, nc.scalar, nc.tensor, nc.gpsimd, nc.sync

    def din(name, shape, dt=F32):
        return nc.dram_tensor(name, shape, dt, kind="ExternalInput").ap()

    x_d = din("x", [SEQ, DM])
    w_in_d = din("w_in", [DM, 2560])
    w_out_d = din("w_out", [DM, DM])
    w_q_d = din("w_query", [DM, 2048])
    g1_d = din("g1", [1, DM])
    g2_d = din("g2", [1, DM])
    gqk_d = din("gqk", [1, DM])
    gsv_d = din("gsv", [1, 512])
    gao_d = din("gao", [1, 512])
    gso_d = din("gso", [1, 512])
    wsT_d = din("wsT", [128, 8, 128])
    tril_d = din("trilT", [128, 128])
    bcol_d = din("bcol", [128, 8])
    keysT_d = din("keysT", [128, 2, 128])
    uT_d = din("uT", [128 * 128, DM])
    ev_d = din("ev", [16384, DM])
    bm_d = din("bm", [128, 3 * 2048])
    ident_d = din("ident", [128, 128])
    iota_d = din("iota", [128, 128])
    c16_d = din("c16", [128, 48])
    out_d = nc.dram_tensor("out", [SEQ, DM], F32, kind="ExternalOutput").ap()

    Vs = nc.dram_tensor("Vs", [SEQ, 520], BF16).ap()
    Gs = nc.dram_tensor("Gs", [SEQ, 512], BF16).ap()
    OB = nc.dram_tensor("OB", [3, SEQ, 520], F32).ap()
    UTb = nc.dram_tensor("UTb", [128 * 128, DM], BF16).ap()
    EVb = nc.dram_tensor("EVb", [16384, DM], BF16).ap()
    H2S = nc.dram_tensor("H2S", [NT, 128, DM], BF16).ap()
    QS = nc.dram_tensor("QS", [NT, 128, 2048], BF16).ap()

    BIG = nc.alloc_sbuf_tensor("BIG", [128, 104000], BF16)
    arena = {'segs': None}

    def set_arena(*segs):
        arena['segs'] = [list(sg) for sg in segs]

    def sb(name, shape, dt=F32):
        esz = 2 if dt == BF16 else 4
        n = 1
        for d_ in shape[1:]:
            n *= d_
        nbytes = (n * esz + 31) // 32 * 32
        for sg in arena['segs']:
            if sg[0] + nbytes <= sg[1]:
                off = sg[0]
                sg[0] += nbytes
                break
        else:
            raise RuntimeError("arena full for %s %s" % (name, shape))
        ap = BIG[:, off // 2: off // 2 + n * esz // 2]
        if dt != BF16:
            ap = ap.bitcast(dt)
        if len(shape) == 3:
            ap = ap.rearrange("p (a b) -> p a b", a=shape[1])
        elif len(shape) == 4:
            ap = ap.rearrange("p (a b c) -> p a b c", a=shape[1], b=shape[2])
        return ap

    A_GLOB = (0, 13312)
    A_RA = (13312, 78848)
    A_RC = (78848, 119808)
    A_MIX = (119808, 141312)
    A_WORK = (141312, 208000)
    A_GT = (78848, 144384)
    A_ET = (144384, 168960)
    A_FG = (168960, 208000)
    A_F2 = (119808, 144384)

    PS = nc.alloc_psum_tensor("PS", [128, 4096], F32)

    def bank(b, n=1):
        return PS[:, b * 512:(b + n) * 512]

    def bank_bf(b):
        return bank(b).bitcast(BF16).rearrange("p (a b) -> p a b", a=8)

    def dma(eng, out, in_, reads=(), writes=()):
        q = SPQ if eng == 'sp' else G
        return S.op(eng, lambda: q.dma_start(out=out, in_=in_), reads, writes, dma=True)

    def dve(fn, reads=(), writes=()):
        return S.op('dve', fn, reads, writes)

    def act(fn, reads=(), writes=()):
        return S.op('act', fn, reads, writes)

    def pe(fn, reads=(), writes=()):
        return S.op('pe', fn, reads, writes)

    def pool(fn, reads=(), writes=()):
        return S.op('pool', fn, reads, writes)

    set_arena(A_RA)
    RA = sb("RA", [128, 8, SEQ], BF16)
    qkT = RA
    set_arena(A_RC)
    RC = sb("RC", [128, 20480], BF16)
    W1 = RC.rearrange("p (c n) -> p c n", c=8)
    bm = RC[:, 0:12288].bitcast(F32)
    Wo = RC[:, 0:8192].rearrange("p (c n) -> p c n", c=8)
    Wq = RC[:, 0:16384].rearrange("p (c n) -> p c n", c=8)

    set_arena(A_GLOB)
    ident_f = sb("ident_f", [128, 128])
    ident_b = sb("ident_b", [128, 128], BF16)
    iota_f = sb("iota_f", [128, 128])
    c16 = sb("c16", [128, 48])
    keysT = sb("keysT", [128, 2, 128], BF16)
    eps_t = sb("eps_t", [128, 1])
    ss = sb("ss", [128, 4])
    sd = sb("sd", [128, 4])
    rs = sb("rs", [128, 4])
    ss16 = sb("ss16", [128, 16])
    sd16 = sb("sd16", [128, 16])
    r16 = sb("r16", [128, 16])
    junk = sb("junk", [128, DM], BF16)
    xts = [sb("xt%d" % i, [128, DM]) for i in range(2)]
    set_arena(A_MIX)
    gA = sb("gA", [128, DM])
    gqk = sb("gqk", [128, DM])
    gsv = sb("gsv", [128, 512])
    gao = sb("gao", [128, 512])
    gso = sb("gso", [128, 512])
    wsT_f = sb("wsT_f", [128, 8, 128])
    tril = sb("tril", [128, 128])
    WsT = sb("WsT", [128, 8, 128], BF16)
    bcol = sb("bcol", [128, 8])

    dma('sp', ident_f[:], ident_d, writes=['ident_f'])
    dma('pool', ident_b[:], ident_d, writes=['ident_b'])
    w_in_v = w_in_d.rearrange("(c p) n -> p c n", p=128)
    dma('pool', W1[:, :, 0:1280], w_in_v[:, :, 0:1280], writes=['RC'])
    dma('pool', W1[:, :, 1280:2560], w_in_v[:, :, 1280:2560], reads=['RC'], writes=['RCb'])
    dma('sp', iota_f[:], iota_d, writes=['iota_f'])
    dma('sp', c16[:], c16_d, writes=['c16'])
    dma('sp', gA[:], g1_d.partition_broadcast(128), writes=['gA'])
    dma('sp', gqk[:], gqk_d.partition_broadcast(128), writes=['gqk'])
    dma('sp', gsv[:], gsv_d.partition_broadcast(128), writes=['gsv'])
    dma('sp', gao[:], gao_d.partition_broadcast(128), writes=['gao'])
    dma('sp', gso[:], gso_d.partition_broadcast(128), writes=['gso'])
    dma('sp', wsT_f[:], wsT_d, writes=['wsT_f'])
    dma('sp', tril[:], tril_d, writes=['tril'])
    dma('sp', bcol[:], bcol_d, writes=['bcol'])
    dma('pool', keysT[:], keysT_d, writes=['keysT'])
    dve(lambda: V.memset(eps_t[:], EPS), writes=['eps'])
    dve(lambda: V.tensor_tensor(out=WsT[:], in0=wsT_f[:], in1=tril[:, None, :].to_broadcast([128, 8, 128]),
                                op=ALU.mult), reads=['wsT_f', 'tril'], writes=['WsT'])
    NPIECE = 64
    PROWS = 16384 // NPIECE

    def cast_piece(k):
        if stage < 3:
            return
        rs_ = slice(k * PROWS, (k + 1) * PROWS)
        dma('pool', UTb[rs_, :], uT_d[rs_, :], writes=['UTb%d' % k])
        dma('pool', EVb[rs_, :], ev_d[rs_, :], writes=['EVb%d' % k])

    def rms_rstd(src_ap, src_keys, n, col):
        act(lambda: A.activation(out=junk[:, 0:n], in_=src_ap, func=AF.Square, accum_out=ss[:, col:col + 1]),
            reads=src_keys, writes=['junk', 'ss%d' % col])
        act(lambda: A.activation(out=sd[:, col:col + 1], in_=ss[:, col:col + 1], func=AF.Sqrt,
                                 scale=1.0 / n, bias=eps_t[:, 0:1]),
            reads=['ss%d' % col, 'eps'], writes=['sd%d' % col])
        dve(lambda: V.reciprocal(out=rs[:, col:col + 1], in_=sd[:, col:col + 1]),
            reads=['sd%d' % col], writes=['rs%d' % col])

    set_arena(A_WORK)
    hbs = [sb("hb%d" % i, [128, DM], BF16) for i in range(2)]
    hTts = [sb("hTt%d" % i, [128, 8, 128], BF16) for i in range(2)]
    sq = sb("sq", [128, DM])
    qkn = sb("qkn", [128, DM])
    qkb = sb("qkb", [128, DM], BF16)
    vbs = [sb("vb%d" % i, [128, 8, 65], BF16) for i in range(2)]
    ss8 = sb("ss8", [128, 8])
    sd8 = sb("sd8", [128, 8])
    r8 = sb("r8", [128, 8])
    svn = sb("svn", [128, 8, 64], BF16)
    t1 = sb("t1", [128, 512])
    gated = sb("gated", [128, 512])
    gnb = sb("gnb", [128, 512], BF16)
    for i in range(2):
        dve(lambda i=i: V.memset(vbs[i][:], 1.0), writes=['vb%d' % i])

    def ab_F1(i):
        p = i % 2
        xt, kx, hb, khb, hTt, kh = xts[p], 'xt%d' % p, hbs[p], 'hb%d' % p, hTts[p], 'hTt%d' % p
        rms_rstd(xt[:], [kx], DM, 0)
        dve(lambda: V.scalar_tensor_tensor(out=hb[:], in0=xt[:], scalar=rs[:, 0:1], in1=gA[:],
                                           op0=ALU.mult, op1=ALU.mult),
            reads=[kx, 'rs0', 'gA'], writes=[khb])
        pT = bank_bf(0)
        for c in range(8):
            pe(lambda c=c: T.transpose(out=pT[:, c, :], in_=hb[:, c * 128:(c + 1) * 128], identity=ident_b[:]),
               reads=[khb, 'ident_b'], writes=['b0'])
        act(lambda: A.copy(out=hTt[:], in_=pT), reads=['b0'], writes=[kh])

    def ab_P(i):
        p = i % 2
        hTt, kh = hTts[p], 'hTt%d' % p
        dsts = [1, 2, 3, 4 + 2 * p, 5 + 2 * p]
        for j in range(5):
            for c in range(8):
                pe(lambda j=j, c=c: T.matmul(bank(dsts[j]), lhsT=hTt[:, c, :], rhs=W1[:, c, j * 512:(j + 1) * 512],
                                             start=(c == 0), stop=(c == 7)),
                   reads=[kh, 'RC', 'RCb'], writes=['b%d' % dsts[j]])

    def ab_B1(i):
        p = i % 2
        tsl = slice(i * 128, (i + 1) * 128)
        pqk = bank(1, 2)
        psv = bank(5 + 2 * p)
        ksv = 'b%d' % (5 + 2 * p)
        act(lambda: A.activation(out=sq[:], in_=pqk, func=AF.Square), reads=['b1', 'b2'], writes=['sq'])
        dve(lambda: V.tensor_reduce(out=ss16[:], in_=sq[:].rearrange("p (a b) -> p a b", b=64), axis=AX.X, op=ALU.add),
            reads=['sq'], writes=['ss16'])
        act(lambda: A.activation(out=sd16[:], in_=ss16[:], func=AF.Sqrt, scale=1.0 / 64, bias=eps_t[:, 0:1]),
            reads=['ss16', 'eps'], writes=['sd16'])
        dve(lambda: V.reciprocal(out=r16[:], in_=sd16[:]), reads=['sd16'], writes=['r16'])
        dve(lambda: V.tensor_tensor(out=qkn[:].rearrange("p (a b) -> p a b", b=64),
                                    in0=pqk.rearrange("p (a b) -> p a b", b=64),
                                    in1=r16[:].unsqueeze(2).to_broadcast([128, 16, 64]), op=ALU.mult),
            reads=['b1', 'b2', 'r16'], writes=['qkn'])
        vb = vbs[p]
        kv = 'vb%d' % p
        act(lambda: A.copy(out=vb[:, :, 0:64], in_=bank(3).rearrange("p (h e) -> p h e", e=64)),
            reads=['b3'], writes=[kv])
        dma('pool', Vs[tsl, :], vb[:].rearrange("p h e -> p (h e)"), reads=[kv], writes=['Vs'])
        act(lambda: A.activation(out=sq[:, 0:512], in_=psv, func=AF.Square), reads=[ksv], writes=['sq'])
        dve(lambda: V.tensor_reduce(out=ss8[:], in_=sq[:, 0:512].rearrange("p (a b) -> p a b", b=64), axis=AX.X,
                                    op=ALU.add), reads=['sq'], writes=['ss8'])
        act(lambda: A.activation(out=sd8[:], in_=ss8[:], func=AF.Sqrt, scale=1.0 / 64, bias=eps_t[:, 0:1]),
            reads=['ss8', 'eps'], writes=['sd8'])
        dve(lambda: V.reciprocal(out=r8[:], in_=sd8[:]), reads=['sd8'], writes=['r8'])
        dve(lambda: V.tensor_tensor(out=svn[:], in0=psv.rearrange("p (a b) -> p a b", b=64),
                                    in1=r8[:].unsqueeze(2).to_broadcast([128, 8, 64]), op=ALU.mult),
            reads=[ksv, 'r8'], writes=['svn'])

    def ab_B2(i):
        p = i % 2
        tsl = slice(i * 128, (i + 1) * 128)
        psu, pmx = bank(4 + 2 * p), bank(5 + 2 * p)
        ksu, kmx = 'b%d' % (4 + 2 * p), 'b%d' % (5 + 2 * p)
        dve(lambda: V.tensor_tensor(out=qkb[:], in0=qkn[:], in1=gqk[:], op=ALU.mult),
            reads=['qkn', 'gqk'], writes=['qkb'])
        pT2 = bank_bf(0)
        for c in range(8):
            pe(lambda c=c: T.transpose(out=pT2[:, c, :], in_=qkb[:, c * 128:(c + 1) * 128], identity=ident_b[:]),
               reads=['qkb', 'ident_b'], writes=['b0'])
        act(lambda: A.copy(out=qkT[:, :, tsl], in_=pT2), reads=['b0'], writes=['qkT'])
        for g in range(8):
            pe(lambda g=g: T.matmul(pmx[:, g * 64:(g + 1) * 64], lhsT=WsT[:, g, :], rhs=svn[:, g, :],
                                    start=True, stop=True), reads=['WsT', 'svn'], writes=[kmx])
        dve(lambda: V.tensor_tensor(out=t1[:], in0=pmx, in1=gsv[:], op=ALU.mult),
            reads=[kmx, 'gsv'], writes=['t1'])
        dve(lambda: V.tensor_tensor(out=t1[:].rearrange("p (a b) -> p a b", b=64),
                                    in0=t1[:].rearrange("p (a b) -> p a b", b=64),
                                    in1=bcol[:].unsqueeze(2).to_broadcast([128, 8, 64]), op=ALU.add),
            reads=['t1', 'bcol'], writes=['t1'])
        dve(lambda: V.tensor_tensor(out=gated[:], in0=t1[:], in1=psu, op=ALU.mult),
            reads=['t1', ksu], writes=['gated'])
        rms_rstd(gated[:], ['gated'], 512, 1)
        dve(lambda: V.scalar_tensor_tensor(out=gnb[:], in0=gated[:], scalar=rs[:, 1:2], in1=gso[:],
                                           op0=ALU.mult, op1=ALU.mult),
            reads=['gated', 'rs1', 'gso'], writes=['gnb'])
        dma('pool', Gs[tsl, :], gnb[:], reads=['gnb'], writes=['Gs'])

    def ab_load(i):
        dma('sp', xts[i % 2][:], x_d[i * 128:(i + 1) * 128, :], writes=['xt%d' % (i % 2)])

    ab_load(0)
    ab_load(1)
    ab_F1(0)
    ab_P(0)
    for i in range(NT):
        if i + 1 < NT:
            ab_F1(i + 1)
        if i + 2 < NT:
            ab_load(i + 2)
        ab_B1(i)
        ab_B2(i)
        cast_piece(i)
        if i + 1 < NT:
            ab_P(i + 1)

    dma('sp', bm, bm_d, writes=['RC', 'RCb'])
    Sbs = [sb("Sb%d" % i, [128, 2, 1024]) for i in range(2)]
    PTs = [sb("PTt%d" % i, [128, 2, 1024], BF16) for i in range(2)]
    vts = [sb("vt%d" % i, [128, 8, 65], BF16) for i in range(3)]
    Osb = [sb("Osb%d" % i, [128, 520]) for i in range(2)]
    iters = []
    for b, d in enumerate([1, 4, 16]):
        for r in range(d):
            for n in range(32 // d):
                iters.append((b, d, r, n))
    units = []
    for it, (b, d, r, n) in enumerate(iters):
        for w in ([0, 1] if n > 0 else [1]):
            units.append((it, w))

    def d_tok(it):
        b, d, r, n = iters[it]
        q0 = n * 128 * d + r
        return slice(q0, q0 + 127 * d + 1, d)

    def d_V(it):
        cur = it % 3
        dma('sp', vts[cur][:].rearrange("p h e -> p (h e)"), Vs[d_tok(it), :], reads=['Vs'], writes=['vt%d' % cur])

    def d_S(ui):
        it, w = units[ui]
        b, d, r, n = iters[it]
        qtok = d_tok(it)
        k0 = (n - 1 + w) * 128 * d + r
        ktok = slice(k0, k0 + 127 * d + 1, d)
        bb = 2 * (ui % 2)
        for par in range(2):
            for hp in range(4):
                o = bank(bb + par)[:, hp * 128:(hp + 1) * 128]
                pe(lambda o=o, par=par, hp=hp: T.matmul(
                    o, lhsT=qkT[par * 64:(par + 1) * 64, 4 + hp, ktok],
                    rhs=qkT[par * 64:(par + 1) * 64, hp, qtok], start=True, stop=True),
                   reads=['qkT'], writes=['b%d' % (bb + par)])

    def d_BE(ui):
        it, w = units[ui]
        b = iters[it][0]
        bb = 2 * (ui % 2)
        Sb, PTt = Sbs[it % 2], PTs[it % 2]
        ks, kp = 'Sb%d_%d' % (it % 2, w), 'PT%d_%d' % (it % 2, w)
        dve(lambda: V.scalar_tensor_tensor(
            out=Sb[:, w, :], in0=bank(bb, 2), scalar=0.125,
            in1=bm[:, (b * 2 + w) * 1024:(b * 2 + w + 1) * 1024], op0=ALU.mult, op1=ALU.add),
            reads=['b%d' % bb, 'b%d' % (bb + 1), 'RC'], writes=[ks])
        act(lambda: A.activation(out=PTt[:, w, :], in_=Sb[:, w, :], func=AF.Exp), reads=[ks], writes=[kp])

    def d_PV(it):
        b, d, r, n = iters[it]
        ws = [0, 1] if n > 0 else [1]
        cur, prv = it % 3, (it - 1) % 3
        PTt = PTs[it % 2]
        ob = 4 + 2 * (it % 2)
        for h in range(8):
            par, hp = h % 2, h // 2
            o = bank(ob + h // 4)[:, (h % 4) * 65:(h % 4) * 65 + 65]
            for wi, w in enumerate(ws):
                vt = vts[cur] if w == 1 else vts[prv]
                kvt = 'vt%d' % (cur if w == 1 else prv)
                col = (par * 4 + hp) * 128
                pe(lambda o=o, w=w, col=col, vt=vt, h=h, wi=wi: T.matmul(
                    o, lhsT=PTt[:, w, col:col + 128], rhs=vt[:, h, :],
                    start=(wi == 0), stop=(wi == len(ws) - 1)),
                   reads=['PT%d_%d' % (it % 2, w), kvt], writes=['b%d' % (ob + h // 4)])
        osb = Osb[it % 2]
        ko = 'Osb%d' % (it % 2)
        act(lambda: A.copy(out=osb[:, 0:260], in_=bank(ob)[:, 0:260]), reads=['b%d' % ob], writes=[ko])
        act(lambda: A.copy(out=osb[:, 260:520], in_=bank(ob + 1)[:, 0:260]), reads=['b%d' % (ob + 1)], writes=[ko])
        dma('pool', OB[b, d_tok(it), :], osb[:], reads=[ko], writes=['OB'])

    d_V(0)
    d_S(0)
    for ui, (it, w) in enumerate(units):
        if ui + 1 < len(units):
            nit, nw = units[ui + 1]
            if nit != it:
                d_V(nit)
            d_S(ui + 1)
        d_BE(ui)
        if ui + 1 == len(units) or units[ui + 1][0] != it:
            d_PV(it)

    S.barrier()
    dma('pool', Wo, w_out_d.rearrange("(c p) n -> p c n", p=128), writes=['RC'])
    dma('sp', gA[:], g2_d.partition_broadcast(128), writes=['gA'])
    obts, accs, rdens, attns, mixbs, mixTs = [], [], [], [], [], []
    for p in range(2):
        set_arena(A_WORK if p == 0 else (78848 + 16384, 119808))
        if p == 0:
            set_arena(A_WORK)
        obts.append(sb("obt%d" % p, [128, 3, 520]))
        accs.append(sb("acc%d" % p, [128, 520]))
        rdens.append(sb("rden%d" % p, [128, 8, 1]))
        attns.append(sb("attn%d" % p, [128, 512]))
        mixbs.append(sb("mixb%d" % p, [128, DM], BF16))
        mixTs.append(sb("mixT%d" % p, [128, 8, 128], BF16))
        if p == 0:
            x2 = [sb("x2_%d" % i, [128, DM]) for i in range(2)]
            h2b = sb("h2b", [128, DM], BF16)
            if stage >= 2:
                WqE = sb("WqE", [128, 8, 2048], BF16)
                h2t_sb = sb("h2t_sb", [128, 8, 128], BF16)
                qT_sb = sb("qT_sb", [128, 2048], BF16)
    if stage >= 2:
        dma('pool', WqE, w_q_d.rearrange("(c p) n -> p c n", p=128), writes=['WqE'])

    def e_E1(i):
        p = i % 2
        tsl = slice(i * 128, (i + 1) * 128)
        obt, acc, rden, attn, mixb, mixT = obts[p], accs[p], rdens[p], attns[p], mixbs[p], mixTs[p]
        ko, ka, kr, kt, kma, kms, kmt = ('obt%d' % p, 'acc%d' % p, 'rden%d' % p, 'attn%d' % p, 'mixb_a%d' % p,
                                         'mixb_s%d' % p, 'mixT%d' % p)
        acc3 = acc[:].rearrange("p (h e) -> p h e", e=65)
        dma('sp', obt[:], OB[:, tsl, :].rearrange("b t c -> t b c"), reads=['OB'], writes=[ko])
        dma('sp', mixb[:, 512:1024], Gs[tsl, :], reads=['Gs'], writes=[kms])
        dve(lambda: V.tensor_tensor(out=acc[:], in0=obt[:, 0, :], in1=obt[:, 1, :], op=ALU.add),
            reads=[ko], writes=[ka])
        dve(lambda: V.tensor_tensor(out=acc[:], in0=acc[:], in1=obt[:, 2, :], op=ALU.add),
            reads=[ko, ka], writes=[ka])
        dve(lambda: V.reciprocal(out=rden[:], in_=acc3[:, :, 64:65]), reads=[ka], writes=[kr])
        dve(lambda: V.tensor_tensor(out=attn[:].rearrange("p (h e) -> p h e", e=64), in0=acc3[:, :, 0:64],
                                    in1=rden[:].to_broadcast([128, 8, 64]), op=ALU.mult),
            reads=[ka, kr], writes=[kt])
        rms_rstd(attn[:], [kt], 512, 2)
        dve(lambda: V.scalar_tensor_tensor(out=mixb[:, 0:512], in0=attn[:], scalar=rs[:, 2:3], in1=gao[:],
                                           op0=ALU.mult, op1=ALU.mult),
            reads=[kt, 'rs2', 'gao'], writes=[kma])
        pT = bank_bf(0)
        for c in range(8):
            pe(lambda c=c: T.transpose(out=pT[:, c, :], in_=mixb[:, c * 128:(c + 1) * 128], identity=ident_b[:]),
               reads=[kma if c < 4 else kms, 'ident_b'], writes=['b0'])
        act(lambda: A.copy(out=mixT[:], in_=pT), reads=['b0'], writes=[kmt])

    def e_E2(i):
        p = i % 2
        tsl = slice(i * 128, (i + 1) * 128)
        mixT, kmt = mixTs[p], 'mixT%d' % p
        xt, kx = xts[p], 'xt%d' % p
        xx, k2 = x2[p], 'x2_%d' % p
        dma('sp', xt[:], x_d[tsl, :], writes=[kx])
        for half in range(2):
            for c in range(8):
                pe(lambda half=half, c=c: T.matmul(bank(1 + half), lhsT=mixT[:, c, :],
                                                   rhs=Wo[:, c, half * 512:(half + 1) * 512],
                                                   start=(c == 0), stop=(c == 7)),
                   reads=[kmt, 'RC'], writes=['b%d' % (1 + half)])
        dve(lambda: V.tensor_tensor(out=xx[:], in0=bank(1, 2), in1=xt[:], op=ALU.add),
            reads=['b1', 'b2', kx], writes=[k2])
        dma('pool', out_d[tsl, :], xx[:], reads=[k2], writes=['out%d' % i])
        if stage >= 2:
            rms_rstd(xx[:], [k2], DM, 3)
            dve(lambda: V.scalar_tensor_tensor(out=h2b[:], in0=xx[:], scalar=rs[:, 3:4], in1=gA[:],
                                               op0=ALU.mult, op1=ALU.mult),
                reads=[k2, 'rs3', 'gA'], writes=['h2b'])
            pT3 = bank_bf(3)
            for c in range(8):
                pe(lambda c=c: T.transpose(out=pT3[:, c, :], in_=h2b[:, c * 128:(c + 1) * 128], identity=ident_b[:]),
                   reads=['h2b', 'ident_b'], writes=['b3'])
            act(lambda: A.copy(out=h2t_sb[:], in_=pT3), reads=['b3'], writes=['h2t_sb'])
            dma('pool', H2S[i], h2t_sb[:].rearrange("p a b -> p (a b)"), reads=['h2t_sb'], writes=['H2S%d' % i])

    def e_E3(i):
        if stage < 2:
            return
        for j in range(16):
            for c in range(8):
                pe(lambda j=j, c=c: T.matmul(bank(4 + j // 4)[:, (j % 4) * 128:(j % 4 + 1) * 128],
                                             lhsT=WqE[:, c, j * 128:(j + 1) * 128], rhs=h2t_sb[:, c, :],
                                             start=(c == 0), stop=(c == 7)),
                   reads=['WqE', 'h2t_sb'], writes=['b%d' % (4 + j // 4)])
        act(lambda: A.copy(out=qT_sb[:], in_=bank(4, 4)), reads=['b4', 'b5', 'b6', 'b7'], writes=['qT_sb'])
        dma('pool', QS[i], qT_sb[:], reads=['qT_sb'], writes=['QS%d' % i])

    e_E1(0)
    for i in range(NT):
        e_E2(i)
        if i + 1 < NT:
            e_E1(i + 1)
        e_E3(i)
        cast_piece(32 + i)

    if stage >= 2:
        S.barrier()
        peer(nc, S, locals())

    S.final_wait('sp')
    S.emit()
    return nc


def peer(nc, S, L):
    V, A, T, G, SPQ = nc.vector, nc.scalar, nc.tensor, nc.gpsimd, nc.sync
    sb, bank, dma, dve, act, pe, pool = L['sb'], L['bank'], L['dma'], L['dve'], L['act'], L['pe'], L['pool']
    keysT = L['keysT']
    ident_f, iota_f, c16 = L['ident_f'], L['iota_f'], L['c16']
    out_d, UTb, EVb, stage, H2S, QS = L['out_d'], L['UTb'], L['EVb'], L['stage'], L['H2S'], L['QS']
    xts = L['xts']
    set_arena = L['set_arena']

    set_arena((13312, 78848))
    GT = sb("GT", [128, TG, 128], BF16)
    set_arena((78848, 208000))
    e1T = sb("e1T", [128, SEQ], BF16)
    e2T = sb("e2T", [128, SEQ], BF16)
    gTt = sb("gTt", [128, SEQ], F32)
    qTs = sb("qTs", [128, 16, 128], BF16)
    ssb = sb("ssb", [128, 16, 128])
    tmp = [sb("tmp%d" % i, [128, 256]) for i in range(2)]
    v16 = sb("v16", [128, 16, 16])
    idx = sb("idx", [128, 16, 16], U32)
    idxf = sb("idxf", [128, 16, 16])
    gq = sb("gq", [128, 4096], BF16)
    cand = gq.bitcast(F32).rearrange("p (h c) -> p h c", h=8)
    CK = ['cand', 'ge4', 'eq4']
    bests = [sb("best%d" % i, [128, 8, 16]) for i in range(2)]
    pos = sb("pos", [128, 8, 16], U32)
    posf = sb("posf", [128, 8, 16])
    eb = sb("eb", [128, 8, 16])
    esum = sb("esum", [128, 8])
    rsum = sb("rsum", [128, 8])
    gts = [sb("gt%d" % i, [128, 128]) for i in range(2)]
    ge4 = gq[:, 0:2048].rearrange("p (a b c) -> p a b c", a=8, b=16)
    eq4 = gq[:, 2048:4096].rearrange("p (a b c) -> p a b c", a=8, b=16)
    r1raw = sb("r1raw", [128, 8, 16])
    r2t = sb("r2t", [128, 8, 16])
    e1s = [sb("e1_%d" % i, [128, 128]) for i in range(2)]
    e2s = [sb("e2_%d" % i, [128, 128]) for i in range(2)]
    thr16 = c16[:, 0:16]
    io16p1 = c16[:, 16:32]
    io16m16 = c16[:, 32:48]
    v4 = v16[:].rearrange("p (h two) r -> p h two r", two=2)
    i4 = idxf[:].rearrange("p (h two) r -> p h two r", two=2)
    cand4 = cand[:].rearrange("p h (a b) -> p h a b", b=16)

    def F_tile(i):
        tsl = slice(i * 128, (i + 1) * 128)
        e1, e2, gt = e1s[i % 2], e2s[i % 2], gts[i % 2]
        best, kbest = bests[i % 2], 'best%d' % (i % 2)
        ke1, ke2, kgt = 'e1_%d' % (i % 2), 'e2_%d' % (i % 2), 'gt%d' % (i % 2)
        dma('sp', qTs[:].rearrange("p a b -> p (a b)"), QS[i], reads=['QS%d' % i], writes=['qTs'])
        yield None
        for qd in range(4):
            for jj in range(4):
                j = qd * 4 + jj
                pe(lambda j=j, jj=jj: T.matmul(bank(7)[:, jj * 128:(jj + 1) * 128], lhsT=qTs[:, j, :],
                                               rhs=keysT[:, j % 2, :], start=True, stop=True),
                   reads=['qTs', 'keysT'], writes=['b7'])
            act(lambda qd=qd: A.copy(out=ssb[:, qd * 4:(qd + 1) * 4, :].rearrange("p a b -> p (a b)"), in_=bank(7)),
                reads=['b7'], writes=['ssb%d' % qd])
            yield None
        for j in range(16):
            tm = tmp[j % 2]
            kt = 'tmp%d' % (j % 2)
            ks = 'ssb%d' % (j // 4)
            dve(lambda j=j: V.max(out=v16[:, j, 0:8], in_=ssb[:, j, :]), reads=[ks], writes=['v16'])
            dve(lambda j=j, tm=tm: V.match_replace(out=tm[:, 0:128], in_to_replace=v16[:, j, 0:8],
                                                   in_values=ssb[:, j, :], imm_value=-1e30),
                reads=[ks, 'v16'], writes=[kt])
            dve(lambda j=j, tm=tm: V.max(out=v16[:, j, 8:16], in_=tm[:, 0:128]), reads=[kt], writes=['v16'])
            dve(lambda j=j: V.max_index(out=idx[:, j, 0:8], in_max=v16[:, j, 0:8], in_values=ssb[:, j, :]),
                reads=[ks, 'v16'], writes=['idx'])
            dve(lambda j=j: V.max_index(out=idx[:, j, 8:16], in_max=v16[:, j, 8:16], in_values=ssb[:, j, :]),
                reads=[ks, 'v16'], writes=['idx'])
            if j % 2 == 1:
                yield None
        dve(lambda: V.tensor_copy(out=idxf[:], in_=idx[:]), reads=['idx'], writes=['idxf'])
        dve(lambda: V.tensor_tensor(out=cand4, in0=v4[:, :, 0, :].unsqueeze(3).to_broadcast([128, 8, 16, 16]),
                                    in1=v4[:, :, 1, :].unsqueeze(2).to_broadcast([128, 8, 16, 16]), op=ALU.add),
            reads=['v16'], writes=CK)
        yield None
        for h in range(8):
            tm = tmp[h % 2]
            kt = 'tmp%d' % (h % 2)
            dve(lambda h=h: V.max(out=best[:, h, 0:8], in_=cand[:, h, :]), reads=CK, writes=[kbest])
            dve(lambda h=h, tm=tm: V.match_replace(out=tm[:], in_to_replace=best[:, h, 0:8], in_values=cand[:, h, :],
                                                   imm_value=-1e30), reads=CK + [kbest], writes=[kt])
            dve(lambda h=h, tm=tm: V.max(out=best[:, h, 8:16], in_=tm[:]), reads=[kt], writes=[kbest])
            dve(lambda h=h: V.max_index(out=pos[:, h, 0:8], in_max=best[:, h, 0:8], in_values=cand[:, h, :]),
                reads=CK + [kbest], writes=['pos'])
            dve(lambda h=h: V.max_index(out=pos[:, h, 8:16], in_max=best[:, h, 8:16], in_values=cand[:, h, :]),
                reads=CK + [kbest], writes=['pos'])
            if h % 2 == 1:
                yield None
        dve(lambda: V.tensor_copy(out=posf[:], in_=pos[:]), reads=['pos'], writes=['posf'])
        dve(lambda: V.tensor_tensor(out=ge4[:], in0=posf[:].unsqueeze(3).to_broadcast([128, 8, 16, 16]),
                                    in1=thr16[:, None, None, :].to_broadcast([128, 8, 16, 16]), op=ALU.is_ge),
            reads=['posf', 'c16'], writes=['ge4'])
        dve(lambda: V.tensor_reduce(out=r1raw[:], in_=ge4[:], axis=AX.X, op=ALU.add), reads=['ge4'], writes=['r1raw'])
        dve(lambda: V.tensor_tensor(out=eq4[:], in0=r1raw[:].unsqueeze(3).to_broadcast([128, 8, 16, 16]),
                                    in1=io16p1[:, None, None, :].to_broadcast([128, 8, 16, 16]), op=ALU.is_equal),
            reads=['r1raw', 'c16'], writes=['eq4'])
        dve(lambda: V.tensor_tensor(out=eq4[:], in0=eq4[:],
                                    in1=i4[:, :, 0, :].unsqueeze(2).to_broadcast([128, 8, 16, 16]), op=ALU.mult),
            reads=['eq4', 'idxf'], writes=['eq4'])
        dve(lambda: V.tensor_reduce(out=e1[:].rearrange("p (h k) -> p h k", k=16), in_=eq4[:], axis=AX.X, op=ALU.add),
            reads=['eq4'], writes=[ke1])
        dve(lambda: V.scalar_tensor_tensor(out=r2t[:], in0=r1raw[:], scalar=-16.0, in1=posf[:],
                                           op0=ALU.mult, op1=ALU.add), reads=['r1raw', 'posf'], writes=['r2t'])
        dve(lambda: V.tensor_tensor(out=ge4[:], in0=r2t[:].unsqueeze(3).to_broadcast([128, 8, 16, 16]),
                                    in1=io16m16[:, None, None, :].to_broadcast([128, 8, 16, 16]), op=ALU.is_equal),
            reads=['r2t', 'c16'], writes=['ge4'])
        dve(lambda: V.tensor_tensor(out=ge4[:], in0=ge4[:],
                                    in1=i4[:, :, 1, :].unsqueeze(2).to_broadcast([128, 8, 16, 16]), op=ALU.mult),
            reads=['ge4', 'idxf'], writes=['ge4'])
        dve(lambda: V.tensor_reduce(out=e2[:].rearrange("p (h k) -> p h k", k=16), in_=ge4[:], axis=AX.X, op=ALU.add),
            reads=['ge4'], writes=[ke2])
        yield 'tail'
        dve(lambda: V.tensor_tensor(out=eb[:], in0=best[:], in1=best[:, :, 0:1].to_broadcast([128, 8, 16]),
                                    op=ALU.subtract), reads=[kbest], writes=['eb'])
        act(lambda: A.activation(out=eb[:], in_=eb[:], func=AF.Exp), reads=['eb'], writes=['eb'])
        dve(lambda: V.tensor_reduce(out=esum[:], in_=eb[:], axis=AX.X, op=ALU.add), reads=['eb'], writes=['esum'])
        dve(lambda: V.reciprocal(out=rsum[:], in_=esum[:]), reads=['esum'], writes=['rsum'])
        dve(lambda: V.tensor_tensor(out=gt[:].rearrange("p (h k) -> p h k", k=16), in0=eb[:],
                                    in1=rsum[:].unsqueeze(2).to_broadcast([128, 8, 16]), op=ALU.mult),
            reads=['eb', 'rsum'], writes=[kgt])
        for m, (src, ks) in enumerate([(e1, ke1), (e2, ke2), (gt, kgt)]):
            pe(lambda m=m, src=src: T.transpose(out=bank(7)[:, m * 128:(m + 1) * 128], in_=src[:], identity=ident_f[:]),
               reads=[ks, 'ident_f'], writes=['b7'])
        act(lambda tsl=tsl: A.copy(out=e1T[:, tsl], in_=bank(7)[:, 0:128]), reads=['b7'], writes=['e1T%d' % i])
        act(lambda tsl=tsl: A.copy(out=e2T[:, tsl], in_=bank(7)[:, 128:256]), reads=['b7'], writes=['e2T%d' % i])
        act(lambda tsl=tsl: A.copy(out=gTt[:, tsl], in_=bank(7)[:, 256:384]), reads=['b7'], writes=['gT%d' % i])

    def exhaust(gen):
        for _ in gen:
            pass

    if stage < 3:
        for i in range(NT):
            exhaust(F_tile(i))
        return

    SBK = 16
    Xs = [sb("X%d" % i, [128, SBK, 128], BF16) for i in range(2)]
    Yts = [sb("Yt%d" % i, [128, SBK, 128], BF16) for i in range(2)]
    Ys = [sb("Y%d" % i, [128, SBK, 128], BF16) for i in range(2)]
    NB = 6
    uts = [sb("ut%d" % i, [128, 8, 128], BF16) for i in range(NB)]
    vcs = [sb("vc%d" % i, [128, DM], BF16) for i in range(NB)]
    ges = [sb("ge%d" % i, [128, TG], BF16) for i in range(3)]
    cts = [sb("ct%d" % i, [128, TG], BF16) for i in range(3)]
    res = Yts[1].rearrange("p a b -> p (a b)").bitcast(F32)
    h2g = [sb("h2g%d" % i, [128, 8, TG], BF16) for i in range(2)]
    NG = SEQ // TG
    ROWS = 16384 // 64
    blk = 0
    gstate = {'gcnt': 0}

    def load_h2g(g):
        for tt in range(2):
            dma('sp', h2g[g % 2][:, :, tt * 128:(tt + 1) * 128],
                H2S[2 * g + tt].rearrange("p (a b) -> p a b", a=8),
                reads=['H2S%d' % (2 * g + tt)], writes=['h2g%d' % (g % 2)])

    exhaust(F_tile(0))
    exhaust(F_tile(1))
    load_h2g(0)
    for g in range(NG):
        t0 = g * TG
        nblk_g = TG // SBK

        def gb_s1(s_):
            bi = blk + s_
            ts0 = t0 + s_ * SBK
            ti = ts0 // 128
            X, Yt = Xs[bi % 2], Yts[bi % 2]
            dve(lambda: V.tensor_tensor(
                out=X[:], in0=iota_f[:, None, :].to_broadcast([128, SBK, 128]),
                in1=e1T[:, ts0:ts0 + SBK].unsqueeze(2).to_broadcast([128, SBK, 128]), op=ALU.is_equal),
                reads=['iota_f', 'e1T%d' % ti], writes=['X%d' % (bi % 2)])
            dve(lambda: V.tensor_tensor(
                out=Yt[:], in0=iota_f[:, None, :].to_broadcast([128, SBK, 128]),
                in1=e2T[:, ts0:ts0 + SBK].unsqueeze(2).to_broadcast([128, SBK, 128]), op=ALU.is_equal),
                reads=['iota_f', 'e2T%d' % ti], writes=['Yt%d' % (bi % 2)])

        def gb_s23(s_):
            nonlocal_gcnt = gstate['gcnt']
            bi = blk + s_
            ts0 = t0 + s_ * SBK
            ti = ts0 // 128
            X, Yt, Y = Xs[bi % 2], Yts[bi % 2], Ys[bi % 2]
            kX, kYt = 'X%d' % (bi % 2), 'Yt%d' % (bi % 2)
            for tt in range(10):
                act(lambda tt=tt: A.activation(out=Y[:, tt, :], in_=Yt[:, tt, :], func=AF.Copy,
                                               scale=gTt[:, ts0 + tt:ts0 + tt + 1]),
                    reads=[kYt, 'gT%d' % ti], writes=['Y%d_%d' % (bi % 2, tt)])
            dve(lambda: V.tensor_tensor(out=Y[:, 10:16, :], in0=Yt[:, 10:16, :],
                                        in1=gTt[:, ts0 + 10:ts0 + 16].unsqueeze(2).to_broadcast([128, 6, 128]),
                                        op=ALU.mult),
                reads=[kYt, 'gT%d' % ti], writes=['Y%d_%d' % (bi % 2, tt) for tt in range(10, 16)])
            for oct_ in range(SBK // 8):
                gc = gstate['gcnt']
                pb = 4 + 2 * (gc % 2)
                for u in range(8):
                    tt = oct_ * 8 + u
                    o = bank(pb + u // 4)[:, (u % 4) * 128:(u % 4 + 1) * 128]
                    pe(lambda o=o, tt=tt: T.matmul(o, lhsT=Y[:, tt, :], rhs=X[:, tt, :], start=True, stop=True),
                       reads=[kX, 'Y%d_%d' % (bi % 2, tt)], writes=['b%d' % (pb + u // 4)])
                tl = s_ * SBK + oct_ * 8
                if True:
                    act(lambda pb=pb, tl=tl: A.copy(out=GT[:, tl:tl + 8, :].rearrange("p t i -> p (t i)"),
                                                    in_=bank(pb, 2)),
                        reads=['b%d' % pb, 'b%d' % (pb + 1)], writes=['GT'])
                else:
                    dve(lambda pb=pb, tl=tl: V.tensor_copy(out=GT[:, tl:tl + 8, :].rearrange("p t i -> p (t i)"),
                                                          in_=bank(pb, 2)),
                        reads=['b%d' % pb, 'b%d' % (pb + 1)], writes=['GT'])
                gstate['gcnt'] += 1

        if not gstate.get('pre_s1'):
            gb_s1(0)
        gstate['pre_s1'] = False
        for s_ in range(nblk_g):
            if s_ + 1 < nblk_g:
                gb_s1(s_ + 1)
            gb_s23(s_)
        blk += nblk_g

        hg = h2g[g % 2]
        khg = 'h2g%d' % (g % 2)

        def load_chunk(c):
            ut, vc = uts[c % NB], vcs[c % NB]
            dma('sp', ut[:].rearrange("p a b -> p (a b)"), UTb[c * 128:(c + 1) * 128, :],
                reads=['UTb%d' % (c * 128 // ROWS)], writes=['ut%d' % (c % NB)])
            dma('sp', vc[:], EVb[c * 128:(c + 1) * 128, :],
                reads=['EVb%d' % (c * 128 // ROWS)], writes=['vc%d' % (c % NB)])

        def a_mm(c, hg=hg, khg=khg):
            ut = uts[c % NB]
            pa = bank(4 + c % 3)[:, 0:TG]
            for dc in range(8):
                pe(lambda ut=ut, pa=pa, dc=dc, hg=hg: T.matmul(pa, lhsT=ut[:, dc, :], rhs=hg[:, dc, :],
                                                               start=(dc == 0), stop=(dc == 7)),
                   reads=['ut%d' % (c % NB), khg], writes=['b%d' % (4 + c % 3)])

        gens = [F_tile(2 * g + 2), F_tile(2 * g + 3)] if g + 1 < NG else []
        tails = []
        state = {'cur': 0}

        def step():
            while state['cur'] < len(gens):
                try:
                    r = next(gens[state['cur']])
                except StopIteration:
                    state['cur'] += 1
                    continue
                if r == 'tail':
                    tails.append(gens[state['cur']])
                    state['cur'] += 1
                    continue
                return

        for c0 in range(5):
            load_chunk(c0)
        def gelu_mult(c):
            gel, ct = ges[c % 3], cts[c % 3]
            act(lambda: A.activation(out=gel[:], in_=bank(4 + c % 3)[:, 0:TG], func=AF.Gelu),
                reads=['b%d' % (4 + c % 3)], writes=['ge%d' % (c % 3)])
            pool(lambda: G.tensor_tensor(out=ct[:], in0=gel[:], in1=GT[:, :, c], op=ALU.mult),
                 reads=['ge%d' % (c % 3), 'GT'], writes=['ct%d' % (c % 3)])

        a_mm(0)
        a_mm(1)
        gelu_mult(0)
        for c in range(128):
            if c + 5 < 128:
                load_chunk(c + 5)
            if c + 2 < 128:
                a_mm(c + 2)
            if c + 1 < 128:
                gelu_mult(c + 1)
            ct, vc = cts[c % 3], vcs[c % NB]
            for tt in range(2):
                for dh in range(2):
                    pe(lambda ct=ct, vc=vc, tt=tt, dh=dh, c=c: T.matmul(
                        bank(tt * 2 + dh), lhsT=ct[:, tt * 128:(tt + 1) * 128], rhs=vc[:, dh * 512:(dh + 1) * 512],
                        start=(c == 0), stop=(c == 127)),
                       reads=['ct%d' % (c % 3), 'vc%d' % (c % NB)], writes=['b%d' % (tt * 2 + dh)])
            if c >= 4 and c % 2 == 0 and c < 120:
                step()
            if c == 64 and g + 1 < NG:
                load_h2g(g + 1)
            if c == 120:
                while state['cur'] < len(gens):
                    step()
                for tg_ in tails:
                    exhaust(tg_)
        if g + 1 < NG:
            t0 = (g + 1) * TG
            gb_s1(0)
            gstate['pre_s1'] = True
        for tt in range(2):
            i = 2 * g + tt
            tsl = slice(i * 128, (i + 1) * 128)
            xt = xts[tt]
            kx = 'xt%d' % tt
            dma('sp', xt[:], out_d[tsl, :], reads=['out%d' % i], writes=[kx])
            dve(lambda xt=xt, tt=tt: V.tensor_tensor(out=res[:], in0=bank(tt * 2, 2), in1=xt[:], op=ALU.add),
                reads=['b%d' % (tt * 2), 'b%d' % (tt * 2 + 1), kx], writes=['Yt1'])
            dma('sp', out_d[tsl, :], res[:], reads=['Yt1'], writes=['out%d' % i])


_CACHE = {}


def host_consts():
    k = np.arange(128)[:, None]
    q = np.arange(128)[None, :]
    bm = np.zeros((128, 3, 2, 2, 4, 128), np.float32)
    for b, d in enumerate([1, 4, 16]):
        for w in range(2):
            dist = (q - k) if w == 1 else (128 + q - k)
            valid = (dist >= 0) if w == 1 else (dist <= 128)
            for h in range(8):
                slope = 2.0 ** (-(h + 1))
                val = np.where(valid, -slope * d * dist, NEG).astype(np.float32)
                bm[:, b, w, h % 2, h // 2, :] = val
    ident = np.eye(128, dtype=np.float32)
    iota = np.broadcast_to(np.arange(128, dtype=np.float32)[None, :], (128, 128)).copy()
    r = np.arange(16, dtype=np.float32)
    c16 = np.broadcast_to(np.concatenate([16 * r, r + 1, r - 16])[None, :], (128, 48)).copy()
    s = np.arange(128)[:, None]
    t = np.arange(128)[None, :]
    trilT = (t >= s).astype(np.float32)
    return dict(bm=bm.reshape(128, 6144), ident=ident, iota=iota, c16=c16, trilT=trilT)


def make_in_maps(inputs, stage=9):
    f = lambda a: np.ascontiguousarray(np.asarray(a, dtype=np.float32))
    x = f(inputs['x'])
    cs = host_consts()
    eu = f(inputs['expert_u'])[0]
    uT = np.ascontiguousarray(eu.reshape(128, 128, 8, 128).transpose(0, 3, 2, 1)).reshape(128 * 128, DM)
    shared = dict(
        w_in=f(inputs['w_in'])[0], w_out=f(inputs['w_out'])[0], w_query=f(inputs['w_query'])[0],
        g1=f(inputs['norm1_g']).reshape(1, DM), g2=f(inputs['norm2_g']).reshape(1, DM),
        gqk=np.concatenate([np.tile(f(inputs['q_norm_g'])[0], 8), np.tile(f(inputs['k_norm_g'])[0], 8)]).reshape(1, DM),
        gsv=f(inputs['sgu_norm_g']).reshape(1, 512), gao=f(inputs['attn_out_g']).reshape(1, 512),
        gso=f(inputs['sgu_out_g']).reshape(1, 512),
        wsT=np.ascontiguousarray(f(inputs['sgu_w'])[0].transpose(2, 0, 1)),
        bcol=np.ascontiguousarray(f(inputs['sgu_b'])[0].T),
        keysT=np.ascontiguousarray(f(inputs['sub_keys'])[0].transpose(2, 0, 1)),
        uT=uT, ev=f(inputs['expert_v'])[0], **cs)
    return [dict(x=np.ascontiguousarray(x[b]), **shared) for b in range(x.shape[0])]


def kernel(**inputs):
    if 'nc' not in _CACHE:
        _CACHE['nc'] = build_program()
    nc = _CACHE['nc']
    in_maps = make_in_maps(inputs)
    n = len(in_maps)
    res = run_bass_kernel_spmd(nc, in_maps, core_ids=list(range(n)))
    return np.stack([np.asarray(r["out"], dtype=np.float32) for r in res.results], axis=0)
```
